# Optimizing a Trainium2 kernel written in Bass

```python
import jax
import jax.numpy as jnp
from jax import lax
import numpy as np

D_MODEL = 1024
BATCH = 32
SEQ = 256
DEPTH = 4
DEC_BATCH = 4
DEC_SEQ = 1024
PAST_LEN = 256

GRID_W = 64
HEAD_DIM = 64
GROUP_W = D_MODEL // 4
MIX_W = 4 * GROUP_W
LRU_W = GROUP_W
LRU_BLOCKS = 4
CONV_W = 4
LRU_C = 8.0
GQA_HEADS = GROUP_W // HEAD_DIM
GQA_KV = GQA_HEADS // 2
NAT_HEADS = GROUP_W // HEAD_DIM
NAT_WR = 8
NAT_WC = 16
RET_HEADS = GROUP_W // HEAD_DIM
RET_CHUNK = 128
D_FF = 11 * D_MODEL // 4
Q_BLOCK = 128
ROPE_BASE = 10000.0
EPS = 1e-6
NEG_INF = -1e30
N_MOD = 9
COL_WIDTHS = (LRU_W, LRU_W,
              GQA_HEADS * HEAD_DIM, GQA_KV * HEAD_DIM, GQA_KV * HEAD_DIM,
              GROUP_W, GROUP_W, GROUP_W,
              GROUP_W, GROUP_W, GROUP_W, GROUP_W)
IN_COLS = 2 * LRU_W + (GQA_HEADS + 2 * GQA_KV) * HEAD_DIM + 3 * GROUP_W + 4 * GROUP_W

kernel_name = 'hybrid_diffusion_prefix_trunk_step'

f32 = jnp.float32


def rmsnorm(x, g):
    xf = x.astype(f32)
    y = xf * lax.rsqrt(jnp.mean(xf * xf, axis=-1, keepdims=True) + EPS)
    return (y * g).astype(x.dtype)


def heads(x, n):
    return x.reshape(*x.shape[:-1], n, HEAD_DIM)


def split_columns(h):
    bounds, acc = [], 0
    for w in COL_WIDTHS[:-1]:
        acc += w
        bounds.append(acc)
    return jnp.split(h, bounds, axis=-1)


def swiglu(x, w_in, w_out):
    a, b = jnp.split(x @ w_in, 2, axis=-1)
    return (jax.nn.silu(a) * b) @ w_out


def adaln(cvec, w_mod, b_mod):
    m = jax.nn.silu(cvec) @ w_mod + b_mod
    return jnp.split(m[:, None, :], N_MOD, axis=-1)


def axial_rope(T):
    t = jnp.arange(T)
    row = (t // GRID_W).astype(f32)
    col = (t % GRID_W).astype(f32)
    n_freq = HEAD_DIM // 4
    inv = ROPE_BASE ** (-jnp.arange(n_freq, dtype=f32) / n_freq)
    ang = jnp.concatenate([row[:, None] * inv, col[:, None] * inv], axis=-1)
    return jnp.cos(ang), jnp.sin(ang)


def apply_rope(x, cos, sin):
    xf = x.astype(f32)
    x1, x2 = xf[..., 0::2], xf[..., 1::2]
    c, s = cos[None, :, None, :], sin[None, :, None, :]
    return jnp.stack([x1 * c - x2 * s, x1 * s + x2 * c], axis=-1).reshape(x.shape).astype(x.dtype)


def gqa_attend(q, k, v):
    B, T, H, D = q.shape
    KV = k.shape[2]
    G = H // KV
    nb = T // Q_BLOCK
    qb = q.reshape(B, nb, Q_BLOCK, KV, G, D).transpose(1, 0, 2, 3, 4, 5)
    scale = D ** -0.5

    def one_block(qblk):
        s = jnp.einsum('bqkgd,bskd->bkgqs', qblk, k, preferred_element_type=f32) * scale
        p = jax.nn.softmax(s, axis=-1).astype(v.dtype)
        return jnp.einsum('bkgqs,bskd->bqkgd', p, v)

    o = lax.map(one_block, qb)
    return o.transpose(1, 0, 2, 3, 4, 5).reshape(B, T, H * D)


def neighbourhood_attend(q, k, v, k_ctx, v_ctx, bias_tab):
    B, T, H, D = q.shape
    R = T // GRID_W
    wr = min(NAT_WR, R)
    rows = jnp.arange(R)
    rs = jnp.clip(rows - wr // 2, 0, R - wr)
    row_idx = rs[:, None] + jnp.arange(wr)[None, :]
    cols = jnp.arange(GRID_W)
    cs = jnp.clip(cols - NAT_WC // 2, 0, GRID_W - NAT_WC)
    in_win = (cols[None, :] >= cs[:, None]) & (cols[None, :] < cs[:, None] + NAT_WC)
    dr = row_idx - rows[:, None] + (NAT_WR - 1)
    dc = jnp.clip(cols[None, :] - cols[:, None] + (NAT_WC - 1), 0, 2 * NAT_WC - 2)
    bias = bias_tab[:, dr[:, None, :, None], dc[None, :, None, :]]
    qg = q.reshape(B, R, GRID_W, H, D)
    kg = k.reshape(B, R, GRID_W, H, D)[:, row_idx]
    vg = v.reshape(B, R, GRID_W, H, D)[:, row_idx]
    scale = D ** -0.5
    s_loc = jnp.einsum('brqhd,brwkhd->bhrqwk', qg, kg, preferred_element_type=f32) * scale
    s_loc = jnp.where(in_win[None, None, None, :, None, :], s_loc + bias[None].astype(f32), NEG_INF)
    s_ctx = jnp.einsum('brqhd,bshd->bhrqs', qg, k_ctx, preferred_element_type=f32) * scale
    n_loc = wr * GRID_W
    p = jax.nn.softmax(jnp.concatenate([s_loc.reshape(B, H, R, GRID_W, n_loc), s_ctx], axis=-1), axis=-1)
    p = p.astype(v.dtype)
    p_loc = p[..., :n_loc].reshape(B, H, R, GRID_W, wr, GRID_W)
    p_ctx = p[..., n_loc:]
    o = (jnp.einsum('bhrqwk,brwkhd->brqhd', p_loc, vg)
         + jnp.einsum('bhrqs,bshd->brqhd', p_ctx, v_ctx))
    return o.reshape(B, T, H * D)


def conv_centred(x, w, b):
    T = x.shape[1]
    left = CONV_W // 2
    xp = jnp.pad(x, ((0, 0), (left, CONV_W - 1 - left), (0, 0)))
    y = xp[:, 0:T] * w[0]
    for i in range(1, CONV_W):
        y = y + xp[:, i:i + T] * w[i]
    return y + b


def _lin_combine(left, right):
    a_l, b_l = left
    a_r, b_r = right
    return a_l * a_r, a_r * b_l + b_r


def rglru_mixer(xa, ga, conv_w, conv_b, w_r, b_r, w_i, b_i, lam, h0):
    xc = conv_centred(xa, conv_w, conv_b).astype(f32)
    B, T, W = xc.shape
    xb = xc.reshape(B, T, LRU_BLOCKS, W // LRU_BLOCKS)
    hs = []
    for d in range(2):
        r = jax.nn.sigmoid(jnp.einsum('btnc,ncd->btnd', xb, w_r[d].astype(f32)).reshape(B, T, W) + b_r[d])
        i = jax.nn.sigmoid(jnp.einsum('btnc,ncd->btnd', xb, w_i[d].astype(f32)).reshape(B, T, W) + b_i[d])
        log_a = -LRU_C * r * jax.nn.softplus(-lam[d].astype(f32))
        a = jnp.exp(log_a)
        u = jnp.sqrt(-jnp.expm1(2.0 * log_a)) * (i * xc)
        A, Bc = lax.associative_scan(_lin_combine, (a, u), axis=1, reverse=(d == 1))
        hs.append(A * h0[:, d, None, :].astype(f32) + Bc)
    y = (hs[0] + hs[1]) * jax.nn.gelu(ga.astype(f32))
    return y.astype(xa.dtype), hs[0][:, -1], hs[1][:, 0]


def retention_scan(q, k, v, log_g, S0):
    B, T, H, D = q.shape
    n = T // RET_CHUNK
    pos = jnp.arange(RET_CHUNK, dtype=f32)
    diff = pos[:, None] - pos[None, :]
    dmat = jnp.where(diff >= 0, jnp.exp(log_g[:, None, None] * jnp.maximum(diff, 0.0)), 0.0)
    xi = jnp.exp(log_g[None, :] * (pos[:, None] + 1.0))
    zeta = jnp.exp(log_g[None, :] * (RET_CHUNK - 1.0 - pos[:, None]))
    g_chunk = jnp.exp(log_g * RET_CHUNK)

    def chunks(t):
        return t.reshape(B, n, RET_CHUNK, H, D).swapaxes(0, 1)

    def step(S, blk):
        qb, kb, vb = blk
        s = jnp.einsum('bqhd,bkhd->bhqk', qb, kb) * dmat
        o = (jnp.einsum('bhqk,bkhe->bqhe', s, vb)
             + jnp.einsum('bqhd,bhde->bqhe', qb, S) * xi[None, :, :, None])
        S = S * g_chunk[None, :, None, None] + jnp.einsum('bkhd,kh,bkhe->bhde', kb, zeta, vb)
        return S, o

    S, o = lax.scan(step, S0, (chunks(q), chunks(k), chunks(v)))
    return o.swapaxes(0, 1).reshape(B, T, H, D), S


def retention_mixer(qd, kd, vd, gd, decay_logit, gn_g, S0):
    B, T, _ = qd.shape
    q = heads(qd, RET_HEADS).astype(f32)
    k = heads(kd, RET_HEADS).astype(f32) * (HEAD_DIM ** -0.5)
    v = heads(vd, RET_HEADS).astype(f32)
    log_g = jax.nn.log_sigmoid(decay_logit.astype(f32))
    o_f, S_f = retention_scan(q, k, v, log_g[0], S0[:, 0].astype(f32))
    o_b, S_b = retention_scan(jnp.flip(q, 1), jnp.flip(k, 1), jnp.flip(v, 1), log_g[1], S0[:, 1].astype(f32))
    o = o_f + jnp.flip(o_b, 1)
    mu = jnp.mean(o, axis=-1, keepdims=True)
    var = jnp.mean(jnp.square(o - mu), axis=-1, keepdims=True)
    o = ((o - mu) * lax.rsqrt(var + EPS)).reshape(B, T, RET_HEADS * HEAD_DIM) * gn_g
    y = o * jax.nn.silu(gd.astype(f32))
    return y.astype(qd.dtype), S_f, S_b


def trunk_layer(x, cvec, lp, ctx):
    is_ctx = ctx is None
    Bn, T, _ = x.shape
    sh1, sc1, g1, sh2, sc2, g2, sh3, sc3, g3 = adaln(cvec, lp['w_mod'], lp['b_mod'])

    h = rmsnorm(x, lp['norm_g'][0]) * (1.0 + sc1) + sh1
    x = x + 0.5 * g1 * swiglu(h, lp['ffn_w_in'][0], lp['ffn_w_out'][0])

    h = rmsnorm(x, lp['norm_g'][1]) * (1.0 + sc2) + sh2
    xa, ga, qb, kb, vb, qc, kc, vc, qd, kd, vd, gd = split_columns(h @ lp['w_in'])

    h0 = jnp.zeros((Bn, 2, LRU_W), f32) if is_ctx else ctx['lru']
    y_a, h_fwd, h_bwd = rglru_mixer(xa, ga, lp['conv_w'], lp['conv_b'], lp['lru_w_r'], lp['lru_b_r'],
                                    lp['lru_w_i'], lp['lru_b_i'], lp['lru_lambda'], h0)
    qb = rmsnorm(heads(qb, GQA_HEADS), lp['gqa_qn'])
    kb = rmsnorm(heads(kb, GQA_KV), lp['gqa_kn'])
    vb = heads(vb, GQA_KV)
    qc = rmsnorm(heads(qc, NAT_HEADS), lp['nat_qn'])
    kc = rmsnorm(heads(kc, NAT_HEADS), lp['nat_kn'])
    vc = heads(vc, NAT_HEADS)
    if is_ctx:
        y_b = gqa_attend(qb, kb, vb)
        y_c = gqa_attend(qc, kc, vc)
        S0 = jnp.zeros((Bn, 2, RET_HEADS, HEAD_DIM, HEAD_DIM), f32)
    else:
        cos, sin = axial_rope(T)
        k_all = jnp.concatenate([apply_rope(kb, cos, sin), ctx['bk']], axis=1)
        v_all = jnp.concatenate([vb, ctx['bv']], axis=1)
        y_b = gqa_attend(apply_rope(qb, cos, sin), k_all, v_all)
        y_c = neighbourhood_attend(qc, kc, vc, ctx['ck'], ctx['cv'], lp['nat_bias'])
        S0 = ctx['ret']
    y_d, S_f, S_b = retention_mixer(qd, kd, vd, gd, lp['ret_decay'], lp['ret_gn'], S0)

    y = jnp.concatenate([y_a, y_b, y_c, y_d], axis=-1) @ lp['w_out']
    x = x + g2 * y

    h = rmsnorm(x, lp['norm_g'][2]) * (1.0 + sc3) + sh3
    x = x + 0.5 * g3 * swiglu(h, lp['ffn_w_in'][1], lp['ffn_w_out'][1])

    if not is_ctx:
        return x, None
    dt = x.dtype
    new = (kb, vb, kc, vc,
           jnp.stack([h_fwd, h_bwd], axis=1).astype(dt),
           jnp.stack([S_f, S_b], axis=1).astype(dt))
    return x, new


def setup_inputs(seed: int = 0) -> dict:
    key = jax.random.key(seed)
    ks = jax.random.split(key, 40)
    nrm = jax.random.normal
    D = D_MODEL
    bw = LRU_W // LRU_BLOCKS
    u = jax.random.uniform(ks[20], (DEPTH, 2, LRU_W), minval=0.9, maxval=0.999)
    s = u ** (1.0 / LRU_C)
    lru_lambda = jnp.log(s) - jnp.log1p(-s)
    gam_logit = jnp.log(2.0 ** (5.0 + jnp.arange(RET_HEADS, dtype=f32)) - 1.0)
    ret_decay = gam_logit[None, None, :] + 0.05 * nrm(ks[21], (DEPTH, 2, RET_HEADS))
    return {
        'x_prompt': nrm(ks[0], (BATCH, SEQ, D)),
        'x_sample': nrm(ks[1], (DEC_BATCH, DEC_SEQ, D)),
        'cache_b_k': nrm(ks[2], (DEC_BATCH, DEPTH, PAST_LEN, GQA_KV, HEAD_DIM)),
        'cache_b_v': nrm(ks[3], (DEC_BATCH, DEPTH, PAST_LEN, GQA_KV, HEAD_DIM)),
        'cache_c_k': nrm(ks[4], (DEC_BATCH, DEPTH, PAST_LEN, NAT_HEADS, HEAD_DIM)),
        'cache_c_v': nrm(ks[5], (DEC_BATCH, DEPTH, PAST_LEN, NAT_HEADS, HEAD_DIM)),
        'state_lru': 0.5 * nrm(ks[6], (DEC_BATCH, DEPTH, 2, LRU_W)),
        'state_ret': 0.1 * nrm(ks[7], (DEC_BATCH, DEPTH, 2, RET_HEADS, HEAD_DIM, HEAD_DIM)),
        'c': nrm(ks[8], (DEC_BATCH, D)),
        'c_ctx': nrm(ks[9], (D,)),
        'w_mod': 0.5 * nrm(ks[10], (DEPTH, D, N_MOD * D)) * D ** -0.5,
        'b_mod': 0.01 * nrm(ks[11], (DEPTH, N_MOD * D)),
        'norm_g': 1.0 + 0.01 * nrm(ks[12], (DEPTH, 3, D)),
        'ffn_w_in': nrm(ks[13], (DEPTH, 2, D, 2 * D_FF)) * D ** -0.5,
        'ffn_w_out': nrm(ks[14], (DEPTH, 2, D_FF, D)) * D_FF ** -0.5,
        'w_in': nrm(ks[15], (DEPTH, D, IN_COLS)) * D ** -0.5,
        'w_out': nrm(ks[16], (DEPTH, MIX_W, D)) * MIX_W ** -0.5,
        'conv_w': nrm(ks[17], (DEPTH, CONV_W, LRU_W)) * CONV_W ** -0.5,
        'conv_b': 0.01 * nrm(ks[18], (DEPTH, LRU_W)),
        'lru_w_r': nrm(ks[19], (DEPTH, 2, LRU_BLOCKS, bw, bw)) * bw ** -0.5,
        'lru_b_r': 0.01 * nrm(ks[22], (DEPTH, 2, LRU_W)),
        'lru_w_i': nrm(ks[23], (DEPTH, 2, LRU_BLOCKS, bw, bw)) * bw ** -0.5,
        'lru_b_i': 0.01 * nrm(ks[24], (DEPTH, 2, LRU_W)),
        'lru_lambda': lru_lambda,
        'gqa_qn': 1.0 + 0.01 * nrm(ks[25], (DEPTH, HEAD_DIM)),
        'gqa_kn': 1.0 + 0.01 * nrm(ks[26], (DEPTH, HEAD_DIM)),
        'nat_qn': 1.0 + 0.01 * nrm(ks[27], (DEPTH, HEAD_DIM)),
        'nat_kn': 1.0 + 0.01 * nrm(ks[28], (DEPTH, HEAD_DIM)),
        'nat_bias': 0.1 * nrm(ks[29], (DEPTH, NAT_HEADS, 2 * NAT_WR - 1, 2 * NAT_WC - 1)),
        'ret_decay': ret_decay,
        'ret_gn': 1.0 + 0.01 * nrm(ks[30], (DEPTH, RET_HEADS * HEAD_DIM)),
    }


def reference(x_prompt, x_sample, cache_b_k, cache_b_v, cache_c_k, cache_c_v, state_lru, state_ret,
              c, c_ctx, w_mod, b_mod, norm_g, ffn_w_in, ffn_w_out, w_in, w_out, conv_w, conv_b,
              lru_w_r, lru_b_r, lru_w_i, lru_b_i, lru_lambda, gqa_qn, gqa_kn, nat_qn, nat_kn,
              nat_bias, ret_decay, ret_gn):
    y_prompt = x_prompt
    y_sample = x_sample
    c_context = c_ctx[None, :]
    bk_l, bv_l, ck_l, cv_l, lru_l, ret_l = [], [], [], [], [], []
    for l in range(DEPTH):
        lp = dict(w_mod=w_mod[l], b_mod=b_mod[l], norm_g=norm_g[l], ffn_w_in=ffn_w_in[l],
                  ffn_w_out=ffn_w_out[l], w_in=w_in[l], w_out=w_out[l], conv_w=conv_w[l],
                  conv_b=conv_b[l], lru_w_r=lru_w_r[l], lru_b_r=lru_b_r[l], lru_w_i=lru_w_i[l],
                  lru_b_i=lru_b_i[l], lru_lambda=lru_lambda[l], gqa_qn=gqa_qn[l], gqa_kn=gqa_kn[l],
                  nat_qn=nat_qn[l], nat_kn=nat_kn[l], nat_bias=nat_bias[l], ret_decay=ret_decay[l],
                  ret_gn=ret_gn[l])
        y_prompt, (bk, bv, ck, cv, hl, sr) = trunk_layer(y_prompt, c_context, lp, None)
        bk_l.append(bk); bv_l.append(bv); ck_l.append(ck); cv_l.append(cv)
        lru_l.append(hl); ret_l.append(sr)
        ctx = dict(bk=cache_b_k[:, l], bv=cache_b_v[:, l], ck=cache_c_k[:, l], cv=cache_c_v[:, l],
                   lru=state_lru[:, l], ret=state_ret[:, l])
        y_sample, _ = trunk_layer(y_sample, c, lp, ctx)
    new_cache_b_k = jnp.stack(bk_l, axis=1)
    new_cache_b_v = jnp.stack(bv_l, axis=1)
    new_cache_c_k = jnp.stack(ck_l, axis=1)
    new_cache_c_v = jnp.stack(cv_l, axis=1)
    new_state_lru = jnp.stack(lru_l, axis=1)
    new_state_ret = jnp.stack(ret_l, axis=1)
    return (y_prompt, y_sample, new_cache_b_k, new_cache_b_v, new_cache_c_k, new_cache_c_v,
            new_state_lru, new_state_ret)
```

```python
import numpy as np
from contextlib import ExitStack
import concourse.bass as bass
import concourse.mybir as mybir
from concourse.bass_utils import run_bass_kernel_spmd

F32 = mybir.dt.float32
BF16 = mybir.dt.bfloat16
AF = mybir.ActivationFunctionType
ALU = mybir.AluOpType

D = 1024
DEPTH = 4
T = 1024
DFF = 2816
INC = 2816
EPS = 1e-6
ENGS = ("pe", "act", "dve", "pool", "sp")


class Prog:
    def __init__(self, nc, same_engine_sync=True):
        self.nc = nc
        self.ops = []
        self.lastw = {}
        self.readers = {}
        self.same_engine_sync = same_engine_sync
        self.pw = {}
        self.old_skip = True

    muted = False

    def op(self, eng, fn, reads=(), writes=(), dsem=None, ninc=1):
        if self.muted:
            return -1
        pr = [k for k in reads if k.startswith("pb")]
        if pr:
            reads = [k for k in reads if not k.startswith("pb")]
            writes = list(writes) + pr
        i = len(self.ops)
        deps = set()
        raw = set()
        for k in reads:
            w = self.lastw.get(k)
            if w is not None:
                deps.add(w)
                raw.add(w)
        for k in writes:
            w = self.lastw.get(k)
            if w is not None:
                deps.add(w)
                if k in pr:
                    pass
            rs = self.readers.get(k)
            if rs:
                deps.update(rs)
        for k in reads:
            self.readers.setdefault(k, []).append(i)
        for k in writes:
            self.lastw[k] = i
            self.readers[k] = []
        deps.discard(i)
        for k in pr:
            w = self.pw.get(k)
            if w is not None:
                raw.add(w)
        for k in writes:
            if k.startswith("pb") and k not in pr:
                self.pw[k] = i
        raw.discard(i)
        self.ops.append(dict(eng=eng, fn=fn, deps=deps, raw=raw, dsem=dsem, ninc=ninc, idx=i))
        return i

    def _skip(self, p, o):
        if not (p["dsem"] is None and o["dsem"] is None and p["eng"] == o["eng"]):
            return False
        if p["eng"] == "pe" or not self.same_engine_sync:
            return True
        if self.old_skip:
            return False
        return p["idx"] not in o["raw"]

    def emit(self, stack):
        nc = self.nc
        ops = self.ops
        need = [False] * len(ops)
        for o in ops:
            for d in o["deps"]:
                if not self._skip(ops[d], o):
                    need[d] = True
        cnt = {e: 0 for e in ENGS}
        dcnt = {}
        for i, o in enumerate(ops):
            if o["dsem"] is not None:
                dcnt[o["dsem"]] = dcnt.get(o["dsem"], 0) + 16 * o["ninc"]
                o["sig"] = (("d", o["dsem"]), dcnt[o["dsem"]])
            elif need[i]:
                cnt[o["eng"]] += 1
                o["sig"] = (("e", o["eng"]), cnt[o["eng"]])
            else:
                o["sig"] = None
        assert max(cnt.values()) < 60000, cnt
        assert max(dcnt.values()) < 60000, dcnt
        sems = {}
        for e in ENGS:
            sems[("e", e)] = stack.enter_context(nc.semaphore("s_" + e))
        for d in dcnt:
            sems[("d", d)] = stack.enter_context(nc.semaphore("d_" + d))
        self.stats = dict(cnt=cnt, nsem=len(sems), nops=len(ops))
        block = stack.enter_context(nc.Block())
        dfinal = dict(dcnt)

        def run(ename):
            def body(eng):
                waited = {}
                for o in ops:
                    if o["eng"] != ename:
                        continue
                    wl = {}
                    for d in o["deps"]:
                        p = ops[d]
                        if p["sig"] is None or self._skip(p, o):
                            continue
                        s, v = p["sig"]
                        if wl.get(s, 0) < v:
                            wl[s] = v
                    for s, v in wl.items():
                        if waited.get(s, 0) < v:
                            eng.wait_ge(sems[s], v)
                            waited[s] = v
                    ins = o["fn"](eng)
                    if o["sig"] is not None:
                        s, v = o["sig"]
                        if s[0] == "d":
                            for i_ in (ins if isinstance(ins, (list, tuple)) else [ins]):
                                i_.then_inc(sems[s], 16)
                        else:
                            ins.then_inc(sems[s], 1)
                if ename == "sp":
                    for d, v in dfinal.items():
                        eng.wait_ge(sems[("d", d)], v)
            return body

        block.tensor(run("pe"))
        block.scalar(run("act"))
        block.vector(run("dve"))
        block.gpsimd(run("pool"))
        block.sync(run("sp"))


def _const_tables():
    cols = {}
    parts = []
    off = [0]

    def add(name, arr):
        a = np.zeros((128, arr.shape[1]), np.float32)
        a[:arr.shape[0]] = arr
        cols[name] = (off[0], arr.shape[1])
        off[0] += arr.shape[1]
        parts.append(a)

    add("ident", np.eye(128, dtype=np.float32))
    add("J", np.eye(128, dtype=np.float32)[::-1].copy())
    b64 = np.zeros((128, 128), np.float32)
    b64[:64, :64] = 1.0 / 64
    b64[64:, 64:] = 1.0 / 64
    add("b64", b64)
    t = np.arange(T)
    row = (t // 64).astype(np.float32)
    col = (t % 64).astype(np.float32)
    inv = (10000.0 ** (-np.arange(16, dtype=np.float32) / 16)).astype(np.float32)
    ang = np.concatenate([row[:, None] * inv, col[:, None] * inv], -1).astype(np.float32)
    cosT = np.cos(ang).T
    sinT = np.sin(ang).T
    sgn = np.where(np.arange(64) % 2 == 0, -1.0, 1.0)
    CA = np.ones((64, 16), np.float32); SA = np.zeros((64, 16), np.float32)
    CB_ = np.ones((64, 64), np.float32); SB = np.zeros((64, 64), np.float32)
    for d_ in range(64):
        j = d_ // 2
        if j < 16:
            CA[d_] = np.cos(np.arange(16, dtype=np.float32) * inv[j])
            SA[d_] = sgn[d_] * np.sin(np.arange(16, dtype=np.float32) * inv[j])
        else:
            CB_[d_] = np.cos(np.arange(64, dtype=np.float32) * inv[j - 16])
            SB[d_] = sgn[d_] * np.sin(np.arange(64, dtype=np.float32) * inv[j - 16])
    add("CA", np.concatenate([CA, CA], 0))
    add("SA", np.concatenate([SA, SA], 0))
    add("CB", np.concatenate([CB_, CB_], 0))
    add("SB", np.concatenate([SB, SB], 0))
    psw = np.zeros((128, 128), np.float32)
    for k in range(128):
        psw[k, k ^ 1] = 1.0
    add("pswap", psw)
    kk = np.arange(128)[:, None].astype(np.float32)
    qq = np.arange(128)[None, :].astype(np.float32)
    add("diffF", np.maximum(qq - kk, 0.0))
    add("maskF", (qq >= kk).astype(np.float32) / 8.0)
    add("diffB", np.maximum(kk - qq, 0.0))
    add("maskB", (kk >= qq).astype(np.float32) / 8.0)
    add("xif", np.broadcast_to(qq + 1.0, (128, 128)).copy())
    add("xib", np.broadcast_to(128.0 - qq, (128, 128)).copy())
    add("zf", 127.0 - kk)
    add("zb", kk.copy())
    hb = np.zeros((31, 127), np.float32)
    for m in range(31):
        hb[m, m + 48] = 1.0
    hA = np.zeros((31, 256), np.float32)
    hA[:, 0:127] = hb
    hB = np.zeros((31, 256), np.float32)
    hB[:, 128:255] = hb
    add("hA", hA)
    add("hB", hB)
    cc = np.arange(64)
    cs = np.clip(cc - 8, 0, 48)
    inw = (cc[None, :] >= cs[:, None]) & (cc[None, :] < cs[:, None] + 16)
    mk = np.where(inw.T, 0.0, -30000.0).astype(np.float32)
    add("maskC", np.concatenate([mk, mk], 0))
    cst = np.concatenate(parts, 1)
    cb = {}
    pb = []
    ob = [0]

    def addb(name, arr):
        cb[name] = (ob[0], arr.shape[1])
        ob[0] += arr.shape[1]
        pb.append(arr.astype(np.float32))

    addb("ones1024", np.full((128, 128), 1.0 / 1024, np.float32))
    addb("ones", np.ones((128, 64), np.float32))
    cstb = np.concatenate(pb, 1)
    return cst, cols, cstb, cb


_CST, _CCOL, _CSTB, _CBCOL = _const_tables()


def build_program(depth=DEPTH, stop=None, mix="ABCD", same_engine_sync=True):
    nc = bass.Bass("TRN2", target_bir_lowering=False)

    def din(name, shape):
        return nc.dram_tensor(name, list(shape), F32, kind="ExternalInput").ap()

    def dout(name, shape):
        return nc.dram_tensor(name, list(shape), F32, kind="ExternalOutput").ap()

    xin = {"P": din("xp", [T, D]), "S": din("xs", [T, D])}
    cvec = din("cvec", [2, D])
    w_mod = din("w_mod", [DEPTH, D, 9 * D])
    b_mod = din("b_mod", [DEPTH, 9 * D])
    norm_g = din("norm_g", [DEPTH, 3, D])
    ffn_w_in = din("ffn_w_in", [DEPTH, 2, D, 2 * DFF])
    ffn_w_out = din("ffn_w_out", [DEPTH, 2, DFF, D])
    w_in = din("w_in", [DEPTH, D, INC])
    w_out = din("w_out", [DEPTH, D, D])
    cbk = din("cbk", [DEPTH, 256, 128]); cbv = din("cbv", [DEPTH, 256, 128])
    cck = din("cck", [DEPTH, 256, 256]); ccv = din("ccv", [DEPTH, 256, 256])
    slru = din("slru", [DEPTH, 2, 256]); sret = din("sret", [DEPTH, 2, 4, 64, 64])
    conv_w = din("conv_w", [DEPTH, 4, 256]); conv_b = din("conv_b", [DEPTH, 256])
    lru_w_r = din("lru_w_r", [DEPTH, 2, 4, 64, 64]); lru_b_r = din("lru_b_r", [DEPTH, 2, 256])
    lru_w_i = din("lru_w_i", [DEPTH, 2, 4, 64, 64]); lru_b_i = din("lru_b_i", [DEPTH, 2, 256])
    lru_lambda = din("lru_lambda", [DEPTH, 2, 256])
    gqa_qn = din("gqa_qn", [DEPTH, 64]); gqa_kn = din("gqa_kn", [DEPTH, 64])
    nat_qn = din("nat_qn", [DEPTH, 64]); nat_kn = din("nat_kn", [DEPTH, 64])
    nat_bias = din("nat_bias", [DEPTH, 4, 15, 31]); ret_decay = din("ret_decay", [DEPTH, 2, 4])
    ret_gn = din("ret_gn", [DEPTH, 256])
    nbk = dout("nbk", [4, DEPTH, 256, 128]); nbv = dout("nbv", [4, DEPTH, 256, 128])
    nck = dout("nck", [4, DEPTH, 256, 256]); ncv = dout("ncv", [4, DEPTH, 256, 256])
    nlru = dout("nlru", [4, DEPTH, 2, 256]); nret = dout("nret", [4, DEPTH, 2, 4, 64, 64])
    cst = din("cst", list(_CST.shape))
    cstb = din("cstb", list(_CSTB.shape))
    yout = {"P": dout("yp", [T, D]), "S": dout("ys", [T, D])}

    with ExitStack() as st:
        def sb(name, shape, dt=F32):
            return st.enter_context(nc.sbuf_tensor(name, list(shape), dt))

        P = Prog(nc, same_engine_sync=same_engine_sync)
        xres = {"P": sb("xP", [128, 8, T]), "S": sb("xS", [128, 8, T])}
        hbf = sb("hbf", [128, 8, T], BF16)
        BIGW = 15360
        big = sb("big", [128, BIGW])
        bigb = big[:].bitcast(BF16)
        slots = [sb(f"slot{i}", [128, 8, 512], BF16) for i in range(4)]
        cstt = sb("cstt", list(_CST.shape))
        cstbt = sb("cstbt", list(_CSTB.shape), BF16)
        vecs = sb("vecs", [128, DEPTH, 128])
        modc = sb("modc", [128, DEPTH, 72, 2])
        coefA = sb("coefA", [128, DEPTH, 2, 3, 8])
        coefG = sb("coefG", [128, DEPTH, 2, 3, 8])
        rows = sb("rows", [128, 128])
        crow = sb("crow", [16, 128])
        scT = sb("scT", [128, 16], BF16)
        sqb = [sb(f"sq{i}", [128, 512], BF16) for i in range(2)]
        mslots = [sb(f"mslot{i}", [128, 8, 128], BF16) for i in range(2)]
        rstdb = [sb(f"rstd{i}", [128, 512]) for i in range(2)]
        tmpf = [sb(f"tmpf{i}", [128, 512]) for i in range(3)]
        dummy = sb("dmy_t", [128, 2])
        lruc = sb("lruc", [128, DEPTH, 2, 4])
        lrut = sb("lrut", [128, 4])
        psb = [st.enter_context(nc.psum_tensor(f"pb{i}", [128, 512], F32)) for i in range(8)]

        def C(name):
            o, w = _CCOL[name]
            return cstt[:, o:o + w]

        def CB(name):
            o, w = _CBCOL[name]
            return cstbt[:, o:o + w]

        rr = {"mm": 0, "aux": 0, "slot": 0, "ev": 0, "sq": 0, "tmp": 0}

        def bank(cls):
            if cls == "mm":
                i = rr["mm"] % 4
                rr["mm"] += 1
            else:
                i = 4 + rr["aux"] % 4
                rr["aux"] += 1
            return psb[i], f"pb{i}"

        def ew_engine():
            rr["ev"] += 1
            return "act" if rr["ev"] % 2 else "dve"

        def barrier():
            P.op("dve", lambda e: e.memset(dummy[:], 0.0), writes=["big", "dummy"])

        def copy_op(eng, out, in_, reads, writes):
            if eng == "act":
                P.op("act", lambda e: e.activation(out=out, in_=in_, func=AF.Copy), reads=reads, writes=writes)
            else:
                P.op(eng, lambda e: e.tensor_copy(out=out, in_=in_), reads=reads, writes=writes)

        def load_slot(pairs):
            k = rr["slot"] % 4
            rr["slot"] += 1
            sl = slots[k]
            pr = [(f(sl), s) for f, s in pairs]
            P.op("pool", lambda e: [e.dma_start(out=d, in_=s) for d, s in pr],
                 writes=[f"slot{k}"], dsem=f"slot{k}", ninc=len(pr))
            return sl, f"slot{k}"

        def xk(g, c, tb):
            return f"x{g}{c}.{tb}"

        def hk(c, tb):
            return f"h{c}.{tb}"

        P.op("sp", lambda e: e.dma_start(out=cstt[:], in_=cst), writes=["cst"], dsem="cst")
        P.op("pool", lambda e: e.dma_start(out=cstbt[:], in_=cstb), writes=["cstb"], dsem="cstb")
        P.op("dve", lambda e: e.memset(rows[:], 0.0), writes=["rows"])
        ident = C("ident")

        stage = [big[:, 0:1024], big[:, 1024:2048]]
        for g in ("P", "S"):
            for blk in range(8):
                stg = stage[blk % 2]
                sk = f"stg{blk % 2}"
                P.op("sp", (lambda stg, src: lambda e: e.dma_start(out=stg, in_=src))(stg, xin[g][blk * 128:(blk + 1) * 128, :]),
                     reads=["big"], writes=[sk], dsem=sk)
                for half in range(2):
                    pbk, pk = bank("aux")

                    def tr(e, pbk=pbk, stg=stg, half=half):
                        for c4 in range(4):
                            i_ = e.transpose(pbk[:, c4 * 128:(c4 + 1) * 128],
                                             stg[:, (half * 4 + c4) * 128:(half * 4 + c4 + 1) * 128], ident)
                        return i_
                    P.op("pe", tr, reads=[sk, "cst", "big"], writes=[pk])
                    copy_op(ew_engine(), xres[g][:, half * 4:(half + 1) * 4, blk * 128:(blk + 1) * 128],
                            pbk[:].rearrange("p (c t) -> p c t", c=4), [pk],
                            [xk(g, c, blk // 4) for c in range(half * 4, half * 4 + 4)])

        P.op("sp", lambda e: e.dma_start(out=crow[:], in_=cvec.rearrange("v (c p) -> (v c) p", p=128)),
             writes=["crow"], dsem="crow")
        pbk, pk = bank("aux")
        P.op("pe", lambda e, pbk=pbk: e.transpose(pbk[:, 0:16], crow[:], ident[0:16, 0:16]), reads=["crow", "cst"], writes=[pk])
        P.op("act", lambda e, pbk=pbk: e.activation(out=scT[:], in_=pbk[:, 0:16], func=AF.Silu), reads=[pk], writes=["scT"])

        for l in range(depth):
            def ld_rows(e, l=l):
                ins = []
                ins.append(e.dma_start(out=rows[0:72, :], in_=b_mod[l].rearrange("(r p) -> r p", p=128)))
                ins.append(e.dma_start(out=rows[72:96, :], in_=norm_g[l].rearrange("s (c p) -> (s c) p", p=128)))
                ins.append(e.dma_start(out=rows[96:104, :], in_=conv_w[l].rearrange("i (c p) -> (i c) p", p=128)))
                ins.append(e.dma_start(out=rows[104:106, :], in_=conv_b[l].rearrange("(c p) -> c p", p=128)))
                ins.append(e.dma_start(out=rows[106:110, :], in_=lru_b_r[l].rearrange("d (c p) -> (d c) p", p=128)))
                ins.append(e.dma_start(out=rows[110:114, :], in_=lru_b_i[l].rearrange("d (c p) -> (d c) p", p=128)))
                ins.append(e.dma_start(out=rows[114:118, :], in_=lru_lambda[l].rearrange("d (c p) -> (d c) p", p=128)))
                ins.append(e.dma_start(out=rows[118:120, :], in_=ret_gn[l].rearrange("(c p) -> c p", p=128)))
                for i_, v_ in enumerate((gqa_qn, gqa_kn, nat_qn, nat_kn)):
                    ins.append(e.dma_start(out=rows[120 + i_:121 + i_, 0:64], in_=v_[l:l + 1, :]))
                    ins.append(e.dma_start(out=rows[120 + i_:121 + i_, 64:128], in_=v_[l:l + 1, :]))
                ins.append(e.dma_start(out=rows[124:128, :], in_=slru[l].rearrange("d (c p) -> (d c) p", p=128)))
                return ins
            P.op("sp", ld_rows, writes=["rows"], dsem="rows", ninc=17)
            pbk, pk = bank("aux")
            P.op("pe", lambda e, pbk=pbk: e.transpose(pbk[:, 0:128], rows[:], ident), reads=["rows", "cst"], writes=[pk])
            copy_op("dve", vecs[:, l, :], pbk[:, 0:128], [pk], [f"vecs{l}"])
            P.op("act", lambda e, l=l: e.activation(out=lrut[:], in_=vecs[:, l, 114:118], func=AF.Exp, scale=-1.0), reads=[f"vecs{l}"], writes=["lrut"])
            P.op("act", lambda e: e.activation(out=lrut[:], in_=lrut[:], func=AF.Ln, bias=1.0), reads=["lrut"], writes=["lrut"])
            P.op("dve", lambda e, l=l: e.tensor_scalar(out=lruc[:, l, 0, :], in0=lrut[:], scalar1=-8.0, scalar2=None, op0=ALU.mult), reads=["lrut"], writes=["lruc"])
            P.op("dve", lambda e, l=l: e.tensor_scalar(out=lruc[:, l, 1, :], in0=lrut[:], scalar1=-16.0, scalar2=None, op0=ALU.mult), reads=["lrut"], writes=["lruc"])
        def mod_step(l, s):
            mi = rr["ms"] % 2
            rr["ms"] += 1
            sl, sk = mslots[mi], f"mslot{mi}"
            P.op("pool", lambda e: [e.dma_start(out=sl[:, k:k + 4, :],
                                                in_=w_mod[l, k * 128:(k + 4) * 128, s * 128:(s + 1) * 128].rearrange("(c p) n -> p c n", p=128))
                                    for k in (0, 4)], writes=[sk], dsem=sk, ninc=2)
            pm, pmk = bank("aux")

            def mv(e):
                for k in range(8):
                    i_ = e.matmul(pm[:, 0:2], sl[:, k, :], scT[:, k::8], start=(k == 0), stop=(k == 7))
                return i_
            P.op("pe", mv, reads=[sk, "scT"], writes=[pmk])
            P.op("dve", lambda e: e.tensor_tensor(out=modc[:, l, s, :], in0=pm[:, 0:2],
                                                  in1=vecs[:, l, s:s + 1].to_broadcast([128, 2]), op=ALU.add),
                 reads=[pmk, f"vecs{l}"], writes=[f"modc{l}"])
            if s % 24 == 23:
                s3 = s // 24
                for v in range(2):
                    P.op("dve", lambda e, v=v, s3=s3: e.scalar_tensor_tensor(
                        out=coefA[:, l, v, s3, :], in0=modc[:, l, (3 * s3 + 1) * 8:(3 * s3 + 2) * 8, v], scalar=1.0,
                        in1=vecs[:, l, 72 + s3 * 8:72 + s3 * 8 + 8], op0=ALU.add, op1=ALU.mult),
                        reads=[f"modc{l}", f"vecs{l}"], writes=[f"coef{l}"])
                    P.op("dve", lambda e, v=v, s3=s3: e.tensor_scalar(
                        out=coefG[:, l, v, s3, :], in0=modc[:, l, (3 * s3 + 2) * 8:(3 * s3 + 3) * 8, v],
                        scalar1=(1.0 if s3 == 1 else 0.5), scalar2=None, op0=ALU.mult),
                        reads=[f"modc{l}"], writes=[f"coef{l}"])

        rr["ms"] = 0
        for s_ in range(24):
            mod_step(0, s_)
        mod_pending = [(0, s_) for s_ in range(24, 72)] + [(l_, s_) for l_ in range(1, depth) for s_ in range(72)]

        def mod_flush(l, s3):
            while mod_pending and (mod_pending[0][0] < l or (mod_pending[0][0] == l and mod_pending[0][1] < 24 * (s3 + 1))):
                l_, s_ = mod_pending.pop(0)
                mod_step(l_, s_)

        def mod_tick(cur_l):
            for _ in range(2):
                if mod_pending and mod_pending[0][0] <= cur_l + 1:
                    l_, s_ = mod_pending.pop(0)
                    mod_step(l_, s_)

        barrier()

        def norm(g, l, s):
            mod_flush(l, s)
            v = 1 if g == "P" else 0
            for tb in range(2):
                tbs = slice(tb * 512, (tb + 1) * 512)
                pst, pstk = bank("aux")
                for c in range(8):
                    sq = sqb[rr["sq"] % 2]
                    sqk = f"sq{rr['sq'] % 2}"
                    rr["sq"] += 1
                    if c % 2 == 0:
                        P.op("act", lambda e, sq=sq, c=c, tbs=tbs: e.activation(out=sq[:], in_=xres[g][:, c, tbs], func=AF.Square),
                             reads=[xk(g, c, tb)], writes=[sqk])
                    else:
                        P.op("dve", lambda e, sq=sq, c=c, tbs=tbs: e.tensor_tensor(out=sq[:], in0=xres[g][:, c, tbs], in1=xres[g][:, c, tbs], op=ALU.mult),
                             reads=[xk(g, c, tb)], writes=[sqk])
                    P.op("pe", lambda e, sq=sq, c=c, pst=pst: e.matmul(pst[:], CB("ones1024"), sq[:], start=(c == 0), stop=(c == 7)),
                         reads=[sqk, "cstb"], writes=[pstk])
                rstd = rstdb[tb]
                rk = f"rstd{tb}"
                P.op("act", lambda e, rstd=rstd, pst=pst: e.activation(out=rstd[:], in_=pst[:], func=AF.Ln, bias=EPS),
                     reads=[pstk], writes=[rk])
                P.op("act", lambda e, rstd=rstd: e.activation(out=rstd[:], in_=rstd[:], func=AF.Exp, scale=-0.5),
                     reads=[rk], writes=[rk])
                for c in range(8):
                    tmp = tmpf[rr["tmp"] % 3]
                    tk = f"tmpf{rr['tmp'] % 3}"
                    rr["tmp"] += 1
                    P.op("dve", lambda e, tmp=tmp, c=c, rstd=rstd, tbs=tbs: e.scalar_tensor_tensor(
                        out=tmp[:], in0=xres[g][:, c, tbs], scalar=coefA[:, l, v, s, c:c + 1], in1=rstd[:],
                        op0=ALU.mult, op1=ALU.mult), reads=[xk(g, c, tb), rk, f"coef{l}"], writes=[tk])
                    shift = modc[:, l, 3 * s * 8 + c, v:v + 1]
                    if c % 2 == 0:
                        P.op("act", lambda e, tmp=tmp, c=c, shift=shift, tbs=tbs: e.activation(
                            out=hbf[:, c, tbs], in_=tmp[:], func=AF.Identity, bias=shift),
                            reads=[tk, f"modc{l}"], writes=[hk(c, tb)])
                    else:
                        P.op("dve", lambda e, tmp=tmp, c=c, shift=shift, tbs=tbs: e.tensor_scalar(
                            out=hbf[:, c, tbs], in0=tmp[:], scalar1=shift, scalar2=None, op0=ALU.add),
                            reads=[tk, f"modc{l}"], writes=[hk(c, tb)])

        hid = bigb[:, 0:22 * T].rearrange("p (j t) -> p j t", j=22)
        hall = [hk(c, tb) for c in range(8) for tb in range(2)]

        def ffn(g, l, f, s):
            v = 1 if g == "P" else 0
            Wi = ffn_w_in[l, f]
            Wo = ffn_w_out[l, f]
            for jq in range(6):
                nj = 4 if jq < 5 else 2
                j0 = jq * 4
                sa_, sak = load_slot([((lambda sl, k=k: sl[:, k:k + 4, 0:nj * 128]),
                                       Wi[k * 128:(k + 4) * 128, j0 * 128:(j0 + nj) * 128].rearrange("(c p) n -> p c n", p=128))
                                      for k in (0, 4)])
                sb_, sbk = load_slot([((lambda sl, k=k: sl[:, k:k + 4, 0:nj * 128]),
                                       Wi[k * 128:(k + 4) * 128, DFF + j0 * 128:DFF + (j0 + nj) * 128].rearrange("(c p) n -> p c n", p=128))
                                      for k in (0, 4)])
                for tb in range(2):
                    for jj in range(nj):
                        j = j0 + jj
                        tbs = slice(tb * 512, (tb + 1) * 512)
                        pa, pak = bank("mm")
                        pb_, pbk_ = bank("mm")

                        def mma(e, sl=sa_, ps=pa, jj=jj, tbs=tbs):
                            for k in range(8):
                                i_ = e.matmul(ps[:], sl[:, k, jj * 128:(jj + 1) * 128], hbf[:, k, tbs], start=(k == 0), stop=(k == 7))
                            return i_
                        P.op("pe", mma, reads=[sak] + [hk(c, tb) for c in range(8)], writes=[pak])

                        def mmb(e, sl=sb_, ps=pb_, jj=jj, tbs=tbs):
                            for k in range(8):
                                i_ = e.matmul(ps[:], sl[:, k, jj * 128:(jj + 1) * 128], hbf[:, k, tbs], start=(k == 0), stop=(k == 7))
                            return i_
                        P.op("pe", mmb, reads=[sbk] + [hk(c, tb) for c in range(8)], writes=[pbk_])
                        tmp = tmpf[rr["tmp"] % 3]
                        tk = f"tmpf{rr['tmp'] % 3}"
                        rr["tmp"] += 1
                        P.op("act", lambda e, tmp=tmp, pa=pa: e.activation(out=tmp[:], in_=pa[:], func=AF.Silu),
                             reads=[pak], writes=[tk])
                        P.op("dve", lambda e, tmp=tmp, pb_=pb_, j=j, tbs=tbs: e.tensor_tensor(
                            out=hid[:, j, tbs], in0=tmp[:], in1=pb_[:], op=ALU.mult),
                            reads=[tk, pbk_, "big"], writes=[f"hid{j}.{tb}"])
                mod_tick(l)
            for npair in range(4):
                n0 = npair * 2
                sx, sxk = load_slot([((lambda sl, j=j, g_=g_: sl[:, :, :].rearrange("p a (b c) -> p (a b) c", c=256)[:, j:j + g_, :]),
                                      Wo[j * 128:(j + g_) * 128, n0 * 128:(n0 + 2) * 128].rearrange("(c p) n -> p c n", p=128))
                                     for (j, g_) in ((0, 4), (4, 4), (8, 3))])
                sy, syk = load_slot([((lambda sl, j=j, g_=g_: sl[:, :, :].rearrange("p a (b c) -> p (a b) c", c=256)[:, j:j + g_, :]),
                                      Wo[(11 + j) * 128:(11 + j + g_) * 128, n0 * 128:(n0 + 2) * 128].rearrange("(c p) n -> p c n", p=128))
                                     for (j, g_) in ((0, 4), (4, 4), (8, 3))])
                for nn in range(2):
                    n = n0 + nn
                    for tb in range(2):
                        tbs = slice(tb * 512, (tb + 1) * 512)
                        po, pok = bank("mm")

                        def mmo(e, po=po, nn=nn, tbs=tbs, sx=sx, sy=sy):
                            fx = sx[:, :, :].rearrange("p a b -> p (a b)")
                            fy = sy[:, :, :].rearrange("p a b -> p (a b)")
                            for j in range(22):
                                fl = fx if j < 11 else fy
                                jj = j % 11
                                i_ = e.matmul(po[:], fl[:, jj * 256 + nn * 128:jj * 256 + (nn + 1) * 128], hid[:, j, tbs],
                                              start=(j == 0), stop=(j == 21))
                            return i_
                        P.op("pe", mmo, reads=[sxk, syk, "big"] + [f"hid{j}.{tb}" for j in range(22)], writes=[pok])
                        P.op("dve", lambda e, po=po, n=n, tbs=tbs: e.scalar_tensor_tensor(
                            out=xres[g][:, n, tbs], in0=po[:], scalar=coefG[:, l, v, s, n:n + 1], in1=xres[g][:, n, tbs],
                            op0=ALU.mult, op1=ALU.add), reads=[pok, f"coef{l}", xk(g, n, tb)], writes=[xk(g, n, tb)])
                mod_tick(l)

        R0 = 4096
        yT = bigb[:, 0:8 * T].rearrange("p (c t) -> p c t", c=8)
        wblk = sb("wblk", [128, 8, 128], BF16)
        lruo = sb("lruo", [128, 64])
        rvt = [sb(f"rvt{i}", [128, 128]) for i in range(2)]
        lgt = sb("lgt", [128, 8])
        gcht = sb("gcht", [128, 2, 2])
        lgp = sb("lgp", [128, 2, 2])
        ztt = sb("ztt", [128, 8])
        sfm = sb("sfm", [128, 2, 64])
        P.op("dve", lambda e: e.memset(wblk[:], 0.0), writes=["wblk"])
        P.op("dve", lambda e: e.memset(lruo[:], 0.0), writes=["lruo"])
        rr.update(dict(t=0, pt=0, rv=0, so=0))

        def W(off, n):
            return big[:, R0 + off:R0 + off + n]

        def WB(off, n):
            return bigb[:, 2 * (R0 + off):2 * (R0 + off) + n]

        TOFF = 7424
        NTMP = 6

        def tget():
            i = rr["t"] % NTMP
            rr["t"] += 1
            return W(TOFF + i * 512, 512), f"mt{i}"

        def ptget():
            i = rr["pt"] % 3
            rr["pt"] += 1
            return WB(TOFF + NTMP * 512 + i * 256, 512), f"pt{i}"

        def load_w_in(l, col0, ncols):
            return load_slot([((lambda sl, k=k: sl[:, k:k + 4, 0:ncols]),
                               w_in[l, k * 128:(k + 4) * 128, col0:col0 + ncols].rearrange("(c p) n -> p c n", p=128))
                              for k in (0, 4)])

        def proj_fm(sl, slk, co, tb, dup=False):
            pbk, pk = bank("mm")
            tbs = slice(tb * 512, (tb + 1) * 512)

            def f(e):
                for k in range(8):
                    i_ = e.matmul(pbk[:], sl[:, k, co:co + 128], hbf[:, k, tbs], start=(k == 0), stop=(k == 7))
                return i_

            def fdup(e):
                for half in range(2):
                    for k in range(8):
                        i_ = e.matmul(pbk[half * 64:(half + 1) * 64, :], sl[:, k, co:co + 64], hbf[:, k, tbs],
                                      start=(k == 0), stop=(k == 7), tile_position=(0, half * 64))
                return i_
            P.op("pe", fdup if dup else f, reads=[slk] + [hk(c, tb) for c in range(8)], writes=[pk])
            return pbk, pk

        def proj_tm(sl, slk, co, ncols, tok0):
            pbk, pk = bank("mm")

            def f(e):
                for k in range(8):
                    i_ = e.matmul(pbk[:, 0:ncols], hbf[:, k, tok0:tok0 + 128], sl[:, k, co:co + ncols],
                                  start=(k == 0), stop=(k == 7))
                return i_
            tbset = sorted({tok0 // 512, (tok0 + 127) // 512})
            P.op("pe", f, reads=[slk] + [hk(c, tb) for c in range(8) for tb in tbset], writes=[pk])
            return pbk, pk

        def rstd_from(pbk, pk):
            rs, rsk = tget()
            P.op("act", lambda e: e.activation(out=rs, in_=pbk[:], func=AF.Ln, bias=EPS), reads=[pk, "big"], writes=[rsk])
            P.op("act", lambda e: e.activation(out=rs, in_=rs, func=AF.Exp, scale=-0.5), reads=[rsk, "big"], writes=[rsk])
            return rs, rsk

        def hn_A(job):
            pbk, pk = job["proj"]()
            sqf, sqk = tget()
            P.op("act", lambda e: e.activation(out=sqf, in_=pbk[:], func=AF.Square), reads=[pk, "big"], writes=[sqk])
            job.update(pbk=pbk, pk=pk, sqf=sqf, sqk=sqk)

        def hn_B(job):
            pbk, pk, sqf, sqk = job["pbk"], job["pk"], job["sqf"], job["sqk"]
            l, gcol, dst, dstk, tb = job["l"], job["gcol"], job["dst"], job["dstk"], job["tb"]
            ss, ssk = bank("aux")
            P.op("pe", lambda e: e.matmul(ss[:], C("b64"), sqf, start=True, stop=True), reads=[sqk, "cst", "big"], writes=[ssk])
            rs, rsk = rstd_from(ss, ssk)
            if not job.get("rope") and not job.get("want_f32"):
                P.op("dve", lambda e: e.scalar_tensor_tensor(out=dst, in0=pbk[:], scalar=gcol, in1=rs, op0=ALU.mult, op1=ALU.mult),
                     reads=[pk, rsk, f"vecs{l}", "big"], writes=[dstk])
                return
            qn, qnk = tget()
            P.op("dve", lambda e: e.scalar_tensor_tensor(out=qn, in0=pbk[:], scalar=gcol, in1=rs, op0=ALU.mult, op1=ALU.mult),
                 reads=[pk, rsk, f"vecs{l}", "big"], writes=[qnk])
            if job.get("rope"):
                apply_rope((qn, qnk), tb, dst, dstk)
            else:
                P.op("act", lambda e: e.activation(out=dst, in_=qn, func=AF.Copy), reads=[qnk, "big"], writes=[dstk])
            if job.get("post"):
                job["post"]((qn, qnk))

        def run_hn(jobs):
            hn_A(jobs[0])
            for i, j in enumerate(jobs):
                if i + 1 < len(jobs):
                    hn_A(jobs[i + 1])
                hn_B(j)

        def reverse_seq(src, srck, dst, dstk, nseq, L, stg, stgk):
            nb = L // 128
            for half in range(2):
                p1, p1k = bank("aux")

                def trs(e, p1=p1, half=half):
                    for i4 in range(4):
                        b = half * 4 + i4
                        i_ = e.transpose(p1[:, i4 * 128:(i4 + 1) * 128], src[:, b * 128:(b + 1) * 128], ident)
                    return i_
                P.op("pe", trs, reads=[srck, "cst", "big"], writes=[p1k])
                copy_op(ew_engine(), stg[:, half * 512:(half + 1) * 512], p1[:], [p1k, "big"], [stgk])
            for dh in range(2):
                p2, p2k = bank("aux")

                def jm(e, p2=p2, dh=dh):
                    for i4 in range(4):
                        d = dh * 4 + i4
                        s_ = d // nb
                        sb_ = s_ * nb + (nb - 1 - (d - s_ * nb))
                        i_ = e.matmul(p2[:, i4 * 128:(i4 + 1) * 128], stg[:, sb_ * 128:(sb_ + 1) * 128], C("J"), start=True, stop=True)
                    return i_
                P.op("pe", jm, reads=[stgk, "cst", "big"], writes=[p2k])
                copy_op(ew_engine(), dst[:, dh * 512:(dh + 1) * 512], p2[:], [p2k, "big"], [dstk])

        def mixer_A(g, l):
            nseq, L = (4, 256) if g == "P" else (1, 1024)
            if g == "P":
                def ldw(e):
                    ins = []
                    for d in range(2):
                        for c in range(2):
                            for ri, Wt in enumerate((lru_w_r, lru_w_i)):
                                for half in range(2):
                                    ins.append(e.dma_start(out=wblk[half * 64:(half + 1) * 64, d * 4 + c * 2 + ri, half * 64:(half + 1) * 64],
                                                           in_=Wt[l, d, 2 * c + half]))
                    return ins
                P.op("pool", ldw, writes=["wblk"], dsem="wblk", ninc=16)
            sl, slk = load_w_in(l, 0, 512)
            bufs = [(W(i * 1024, 1024), f"lb{i}") for i in range(10)]
            (xa, xak), (ga, gak), (xc, xck), (xcr, xcrk), (B1, B1k), (B2, B2k), (B3, B3k), (B4, B4k), (B5, B5k), (B6, B6k) = bufs
            xcb, xcbk = WB(10240, 1024), "xcb"
            xcrb, xcrbk = WB(10752, 1024), "xcrb"
            for c in range(2):
                for tb in range(2):
                    tbs = slice(tb * 512, (tb + 1) * 512)
                    pbk, pk = proj_fm(sl, slk, c * 128, tb)
                    copy_op("act", xa[:, tbs], pbk[:], [pk, "big"], [xak])
                    pbk2, pk2 = proj_fm(sl, slk, 256 + c * 128, tb)
                    copy_op("dve", ga[:, tbs], pbk2[:], [pk2, "big"], [gak])
                vk = f"vecs{l}"
                P.op("dve", lambda e, c=c: e.tensor_scalar(out=xc, in0=xa, scalar1=vecs[:, l, 100 + c:101 + c], scalar2=vecs[:, l, 104 + c:105 + c],
                                                          op0=ALU.mult, op1=ALU.add), reads=[xak, vk, "big"], writes=[xck])
                xa3 = xa.rearrange("p (s t) -> p s t", s=nseq)
                xc3 = xc.rearrange("p (s t) -> p s t", s=nseq)
                for (i, so, do, n) in ((0, 0, 2, L - 2), (1, 0, 1, L - 1), (3, 1, 0, L - 1)):
                    P.op("dve", lambda e, i=i, so=so, do=do, n=n, c=c: e.scalar_tensor_tensor(
                        out=xc3[:, :, do:do + n], in0=xa3[:, :, so:so + n], scalar=vecs[:, l, 96 + i * 2 + c:96 + i * 2 + c + 1],
                        in1=xc3[:, :, do:do + n], op0=ALU.mult, op1=ALU.add), reads=[xak, xck, vk, "big"], writes=[xck])
                copy_op("act", xcb, xc, [xck, "big"], [xcbk])
                reverse_seq(xc, xck, xcr, xcrk, nseq, L, B1, B1k)
                copy_op("act", xcrb, xcr, [xcrk, "big"], [xcrbk])
                for d in range(2):
                    src, srck, srcb, srcbk = (xc, xck, xcb, xcbk) if d == 0 else (xcr, xcrk, xcrb, xcrbk)
                    A_, A_k = (B1, B1k) if d == 0 else (B5, B5k)
                    U_, U_k = (B2, B2k) if d == 0 else (B6, B6k)
                    H_, H_k = (B3, B3k) if d == 0 else (B4, B4k)
                    dc = d * 2 + c
                    for tb in range(2):
                        tbs = slice(tb * 512, (tb + 1) * 512)
                        for ri, (dst, dstk, bcol) in enumerate(((A_, A_k, 106 + dc), (U_, U_k, 110 + dc))):
                            pbk, pk = bank("mm")
                            P.op("pe", lambda e, pbk=pbk, ri=ri, tbs=tbs, srcb=srcb, d=d, c=c: e.matmul(
                                pbk[:], wblk[:, d * 4 + c * 2 + ri, :], srcb[:, tbs], start=True, stop=True),
                                reads=["wblk", srcbk, "big"], writes=[pk])
                            P.op("act", lambda e, pbk=pbk, dst=dst, tbs=tbs, bcol=bcol: e.activation(
                                out=dst[:, tbs], in_=pbk[:], func=AF.Sigmoid, bias=vecs[:, l, bcol:bcol + 1]),
                                reads=[pk, vk, "big"], writes=[dstk])
                    P.op("act", lambda e, H_=H_, A_=A_, dc=dc: e.activation(out=H_, in_=A_, func=AF.Exp, scale=lruc[:, l, 1, dc:dc + 1]),
                         reads=[A_k, "lruc", "big"], writes=[H_k])
                    P.op("act", lambda e, A_=A_, dc=dc: e.activation(out=A_, in_=A_, func=AF.Exp, scale=lruc[:, l, 0, dc:dc + 1]),
                         reads=[A_k, "lruc", "big"], writes=[A_k])
                    P.op("dve", lambda e, H_=H_: e.tensor_scalar(out=H_, in0=H_, scalar1=-1.0, scalar2=1.0, op0=ALU.mult, op1=ALU.add),
                         reads=[H_k, "big"], writes=[H_k])
                    P.op("act", lambda e, H_=H_: e.activation(out=H_, in_=H_, func=AF.Sqrt), reads=[H_k, "big"], writes=[H_k])
                    P.op("dve", lambda e, U_=U_, src=src: e.tensor_tensor(out=U_, in0=U_, in1=src, op=ALU.mult),
                         reads=[U_k, srck, "big"], writes=[U_k])
                    P.op("dve", lambda e, U_=U_, H_=H_: e.tensor_tensor(out=U_, in0=U_, in1=H_, op=ALU.mult),
                         reads=[U_k, H_k, "big"], writes=[U_k])
                    for s in range(nseq):
                        ss_ = slice(s * L, (s + 1) * L)
                        init = 0.0 if g == "P" else vecs[:, l, 124 + dc:125 + dc]
                        P.op("dve", lambda e, H_=H_, A_=A_, U_=U_, ss_=ss_, init=init: e.tensor_tensor_scan(
                            out=H_[:, ss_], data0=A_[:, ss_], data1=U_[:, ss_], initial=init, op0=ALU.mult, op1=ALU.add),
                            reads=[A_k, U_k, vk, "big"], writes=[H_k])
                        if g == "P":
                            col = ((s * DEPTH + l) * 2 + d) * 2 + c
                            P.op("dve", lambda e, H_=H_, s=s, col=col: e.tensor_copy(out=lruo[:, col:col + 1], in_=H_[:, (s + 1) * L - 1:(s + 1) * L]),
                                 reads=[H_k, "big"], writes=["lruo"])
                reverse_seq(B4, B4k, B1, B1k, nseq, L, B5, B5k)
                P.op("dve", lambda e: e.tensor_tensor(out=B3, in0=B3, in1=B1, op=ALU.add), reads=[B3k, B1k, "big"], writes=[B3k])
                P.op("dve", lambda e: e.tensor_tensor(out=B2, in0=ga, in1=ga, op=ALU.mult), reads=[gak, "big"], writes=[B2k])
                P.op("dve", lambda e: e.tensor_scalar(out=B2, in0=B2, scalar1=0.044715, scalar2=1.0, op0=ALU.mult, op1=ALU.add),
                     reads=[B2k, "big"], writes=[B2k])
                P.op("dve", lambda e: e.tensor_tensor(out=B2, in0=B2, in1=ga, op=ALU.mult), reads=[B2k, gak, "big"], writes=[B2k])
                P.op("act", lambda e: e.activation(out=B2, in_=B2, func=AF.Sigmoid, scale=1.5957691216057308),
                     reads=[B2k, "big"], writes=[B2k])
                P.op("dve", lambda e: e.tensor_tensor(out=B3, in0=B3, in1=ga, op=ALU.mult), reads=[B3k, gak, "big"], writes=[B3k])
                P.op("dve", lambda e, c=c: e.tensor_tensor(out=yT[:, c, :], in0=B3, in1=B2, op=ALU.mult),
                     reads=[B3k, B2k, "big"], writes=[f"y{c}"])

        def rope_tabs(tb):
            tc_, tck = tget()
            ts_, tsk = tget()
            r0 = tb * 8
            P.op("dve", lambda e: e.tensor_tensor(out=tc_.rearrange("p (r c) -> p r c", r=8),
                                                  in0=C("CA")[:, r0:r0 + 8].unsqueeze(2).to_broadcast([128, 8, 64]),
                                                  in1=C("CB").unsqueeze(1).to_broadcast([128, 8, 64]), op=ALU.mult),
                 reads=["cst", "big"], writes=[tck])
            P.op("dve", lambda e: e.tensor_tensor(out=ts_.rearrange("p (r c) -> p r c", r=8),
                                                  in0=C("SA")[:, r0:r0 + 8].unsqueeze(2).to_broadcast([128, 8, 64]),
                                                  in1=C("SB").unsqueeze(1).to_broadcast([128, 8, 64]), op=ALU.add),
                 reads=["cst", "big"], writes=[tsk])
            return (tc_, tck), (ts_, tsk)

        def softmax_attend(qT, kfn, vfn, q_ranges, keylist, ych, extra_reads):
            import os
            if os.environ.get("KDBG_NOATT"):
                return
            for qc in range(2):
                for (q0, qn_) in q_ranges:
                    po, pok = bank("mm")
                    pd, pdk = bank("mm")
                    keys = keylist(q0)
                    items = []
                    for hp in range(2):
                        h = 2 * qc + hp
                        for ki, kd in enumerate(keys):
                            items.append((hp, h, ki, kd))
                    state = {}

                    def emit_sc(i, po=po, pd=pd, q0=q0, qn_=qn_, qc=qc, keys=keys):
                        hp, h, ki, kd = items[i]
                        ps, psk = bank("aux")
                        lhsT = kfn(h, hp, kd)
                        P.op("pe", lambda e, ps=ps, lhsT=lhsT, hp=hp: e.matmul(
                            ps[:, 0:qn_], lhsT, qT[hp * 64:(hp + 1) * 64, qc, q0:q0 + qn_], start=True, stop=True),
                            reads=["mq", "mk", "big"] + extra_reads, writes=[psk])
                        pT, pTk = ptget()
                        P.op("act", lambda e, ps=ps, pT=pT: e.activation(out=pT[:, 0:qn_], in_=ps[:, 0:qn_], func=AF.Exp, scale=0.125),
                             reads=[psk, "big"], writes=[pTk])
                        state[i] = (pT, pTk)

                    def emit_pv(i, po=po, pd=pd, pok=pok, pdk=pdk, qn_=qn_, keys=keys):
                        hp, h, ki, kd = items[i]
                        pT, pTk = state.pop(i)
                        vv = vfn(h, kd)
                        first = (ki == 0)
                        last = (ki == len(keys) - 1)

                        def pv(e):
                            e.matmul(po[hp * 64:(hp + 1) * 64, 0:qn_], vv, pT[:, 0:qn_], start=first, stop=last,
                                     tile_position=(0, hp * 64))
                            return e.matmul(pd[hp * 64:(hp + 1) * 64, 0:qn_], CB("ones"), pT[:, 0:qn_], start=first, stop=last,
                                            tile_position=(0, hp * 64))
                        P.op("pe", pv, reads=[pTk, "mv", "cstb", "big"] + extra_reads, writes=[pok, pdk])
                    emit_sc(0)
                    for i in range(len(items)):
                        if i + 1 < len(items):
                            emit_sc(i + 1)
                        emit_pv(i)
                    rec, reck = tget()
                    P.op("dve", lambda e, rec=rec, pd=pd, qn_=qn_: e.reciprocal(out=rec[:, 0:qn_], in_=pd[:, 0:qn_]),
                         reads=[pdk, "big"], writes=[reck])
                    P.op("dve", lambda e, rec=rec, po=po, qc=qc, q0=q0, qn_=qn_: e.tensor_tensor(
                        out=yT[:, ych + qc, q0:q0 + qn_], in0=po[:, 0:qn_], in1=rec[:, 0:qn_], op=ALU.mult),
                        reads=[pok, reck, "big"], writes=[f"y{ych + qc}"])

        def mixer_attn(g, l, kind):
            isB = kind == "B"
            isP = g == "P"
            qcol0 = 512 if isB else 1024
            kw = 128 if isB else 256
            vw = kw
            ych = 2 if isB else 4
            gq = 120 if isB else 122
            gk = gq + 1
            kout, vout = (nbk, nbv) if isB else (nck, ncv)
            sl, slk = load_w_in(l, qcol0, 256 + kw)
            slv, slvk = load_w_in(l, qcol0 + 256 + kw, vw)
            qT = WB(0, 2048).rearrange("p (c t) -> p c t", c=2)
            kT = WB(1024, 2048).rearrange("p (c t) -> p c t", c=2)
            vtok = WB(2048, 8 * vw).rearrange("p (b c) -> p b c", b=8)
            kst = [W(3200 + i * 1024, 4 * kw).rearrange("p (b c) -> p b c", b=4) for i in range(2)]
            vst = [W(5248 + i * 1024, 4 * vw).rearrange("p (b c) -> p b c", b=4) for i in range(2)]
            rope = not isP
            vk = f"vecs{l}"
            jobs = []
            for tb in range(2):
                tbs = slice(tb * 512, (tb + 1) * 512)
                for qc in range(2):
                    jobs.append(dict(proj=(lambda qc=qc, tb=tb: proj_fm(sl, slk, qc * 128, tb)), l=l, gcol=vecs[:, l, gq:gq + 1], tb=tb,
                                     dst=qT[:, qc, tbs], dstk="mq", rope=rope, want_f32=rope))
                for kc in range(2):
                    def post(r, kc=kc, tb=tb):
                        qn, qnk = r
                        pt_, ptk = bank("aux")
                        wd = 64 if isB else 128

                        def trk(e):
                            for b in range(4):
                                i_ = e.transpose(pt_[:, b * wd:(b + 1) * wd], qn[0:wd, b * 128:(b + 1) * 128], ident[0:wd, 0:wd])
                            return i_
                        P.op("pe", trk, reads=[qnk, "cst", "big"], writes=[ptk])
                        copy_op("dve", kst[tb][:, :, kc * wd:(kc + 1) * wd], pt_[:, 0:4 * wd].rearrange("p (b c) -> p b c", b=4),
                                [ptk, "big"], [f"kst{tb}"])
                        if kc == 1:
                            P.op("sp", lambda e: [e.dma_start(
                                out=kout[2 * tb + s_, l].rearrange("(b p) c -> p b c", p=128), in_=kst[tb][:, 2 * s_:2 * s_ + 2, :]) for s_ in range(2)],
                                reads=[f"kst{tb}", "big"], dsem=f"kst{kind}{tb}", ninc=2)
                    jobs.append(dict(proj=(lambda kc=kc, tb=tb: proj_fm(sl, slk, 256 + (kc * 64 if isB else kc * 128), tb, dup=isB)), l=l,
                                     gcol=vecs[:, l, gk:gk + 1], tb=tb, dst=kT[:, kc, tbs], dstk="mk", rope=rope, want_f32=True,
                                     post=(post if isP else None)))
            run_hn(jobs)
            for blk in range(8):
                tb = blk // 4
                pbk, pk = proj_tm(slv, slvk, 0, vw, blk * 128)
                copy_op("act", vtok[:, blk, :], pbk[:, 0:vw], [pk, "big"], ["mv"])
                if isP:
                    copy_op("dve", vst[tb][:, blk % 4, :], pbk[:, 0:vw], [pk, "big"], [f"vst{tb}"])
                    if blk % 4 == 3:
                        P.op("sp", lambda e, tb=tb: [e.dma_start(
                            out=vout[2 * tb + s_, l].rearrange("(b p) c -> p b c", p=128), in_=vst[tb][:, 2 * s_:2 * s_ + 2, :]) for s_ in range(2)],
                            reads=[f"vst{tb}", "big"], dsem=f"vst{kind}{tb}", ninc=2)
            if isP:
                keylist = lambda q0: [("l", q0 // 128), ("l", q0 // 128 + 1)]
                q_ranges = [(s * 256, 256) for s in range(4)]
                kfn = lambda h, hp, kd: kT[hp * 64:(hp + 1) * 64, h // 2, kd[1] * 128:(kd[1] + 1) * 128]
                vfn = lambda h, kd: vtok[:, kd[1], ((h // 2) if isB else h) * 64:((h // 2) if isB else h) * 64 + 64]
                softmax_attend(qT, kfn, vfn, q_ranges, keylist, ych, [])
            else:
                kctx = WB(3200, 512).rearrange("p (c t) -> p c t", c=2)
                vctx = WB(3456, 256).rearrange("p (b c) -> p b c", b=2)
                cstg = W(3584, 256).rearrange("p (b c) -> p b c", b=2)
                P.op("sp", lambda e: e.dma_start(out=cstg, in_=cbk[l].rearrange("(b p) c -> p b c", p=128)),
                     reads=["big"], writes=["cstg"], dsem="cstg")
                P.op("pool", lambda e: [e.dma_start(out=vctx[:, b, :], in_=cbv[l, b * 128:(b + 1) * 128, :]) for b in range(2)],
                     reads=["big"], writes=["vctx"], dsem="vctx", ninc=2)
                for kv in range(2):
                    pt_, ptk = bank("aux")

                    def trc(e, pt_=pt_, kv=kv):
                        for b in range(2):
                            for half in range(2):
                                i_ = e.matmul(pt_[half * 64:(half + 1) * 64, b * 128:(b + 1) * 128],
                                              cstg[:, b, kv * 64:(kv + 1) * 64], ident, start=True, stop=True,
                                              tile_position=(0, half * 64))
                        return i_
                    P.op("pe", trc, reads=["cstg", "cst", "big"], writes=[ptk])
                    copy_op("act", kctx[:, kv, :], pt_[:, 0:256], [ptk, "big"], ["kctx"])
                keylist = lambda q0: [("l", b) for b in range(8)] + [("c", b) for b in range(2)]
                q_ranges = [(0, 512), (512, 512)]

                def kfn(h, hp, kd):
                    if kd[0] == "l":
                        return kT[hp * 64:(hp + 1) * 64, h // 2, kd[1] * 128:(kd[1] + 1) * 128]
                    return kctx[hp * 64:(hp + 1) * 64, h // 2, kd[1] * 128:(kd[1] + 1) * 128]

                def vfn(h, kd):
                    kv = h // 2
                    if kd[0] == "l":
                        return vtok[:, kd[1], kv * 64:(kv + 1) * 64]
                    return vctx[:, kd[1], kv * 64:(kv + 1) * 64]
                softmax_attend(qT, kfn, vfn, q_ranges, keylist, ych, ["kctx", "vctx"])

        def apply_rope(r, tb, dst, dstk):
            qn, qnk = r
            (tc_, tck), (ts_, tsk) = rope_tabs(tb)
            sw, swk = bank("aux")
            P.op("pe", lambda e: e.matmul(sw[:], C("pswap"), qn, start=True, stop=True), reads=[qnk, "cst", "big"], writes=[swk])
            P.op("dve", lambda e: e.tensor_tensor(out=ts_, in0=sw[:], in1=ts_, op=ALU.mult), reads=[swk, tsk, "big"], writes=[tsk])
            P.op("dve", lambda e: e.tensor_tensor(out=tc_, in0=qn, in1=tc_, op=ALU.mult), reads=[qnk, tck, "big"], writes=[tck])
            P.op("dve", lambda e: e.tensor_tensor(out=dst, in0=tc_, in1=ts_, op=ALU.add), reads=[tck, tsk, "big"], writes=[dstk])

        def mixer_nat(l):
            sl, slk = load_w_in(l, 1024, 512)
            slv, slvk = load_w_in(l, 1536, 256)
            qT = WB(0, 2048).rearrange("p (c t) -> p c t", c=2)
            kT = WB(1024, 2048).rearrange("p (c t) -> p c t", c=2)
            vtok = WB(2048, 2048).rearrange("p (b c) -> p b c", b=8)
            vsh = WB(3072, 1792).rearrange("p (b c) -> p b c", b=7)
            kctx = WB(3968, 512).rearrange("p (c t) -> p c t", c=2)
            vctx = WB(4224, 512).rearrange("p (b c) -> p b c", b=2)
            cstg = W(4480, 512).rearrange("p (b c) -> p b c", b=2)
            bias2 = WB(4992, 3584).rearrange("p (h i c) -> p h i c", h=4, i=14)
            tabr = W(6784, 32)
            tabT = W(6848, 64)
            vk = f"vecs{l}"
            jobs = []
            for tb in range(2):
                tbs = slice(tb * 512, (tb + 1) * 512)
                for qc in range(2):
                    jobs.append(dict(proj=(lambda qc=qc, tb=tb: proj_fm(sl, slk, qc * 128, tb)), l=l, gcol=vecs[:, l, 122:123], tb=tb,
                                     dst=qT[:, qc, tbs], dstk="mq"))
                for kc in range(2):
                    jobs.append(dict(proj=(lambda kc=kc, tb=tb: proj_fm(sl, slk, 256 + kc * 128, tb)), l=l, gcol=vecs[:, l, 123:124], tb=tb,
                                     dst=kT[:, kc, tbs], dstk="mk"))
            run_hn(jobs)
            for blk in range(8):
                pbk, pk = proj_tm(slv, slvk, 0, 256, blk * 128)
                copy_op(ew_engine(), vtok[:, blk, :], pbk[:, 0:256], [pk, "big"], ["mv"])
            for i in range(7):
                pbk, pk = proj_tm(slv, slvk, 0, 256, 64 + i * 128)
                copy_op(ew_engine(), vsh[:, i, :], pbk[:, 0:256], [pk, "big"], ["mv"])
            P.op("sp", lambda e: e.dma_start(out=cstg, in_=cck[l].rearrange("(b p) c -> p b c", p=128)),
                 reads=["big"], writes=["cstg"], dsem="cstg")
            P.op("pool", lambda e: [e.dma_start(out=vctx[:, b, :], in_=ccv[l, b * 128:(b + 1) * 128, :]) for b in range(2)],
                 reads=["big"], writes=["vctx"], dsem="vctx", ninc=2)
            for ch in range(2):
                pt_, ptk = bank("aux")

                def trc(e, pt_=pt_, ch=ch):
                    for b in range(2):
                        i_ = e.transpose(pt_[:, b * 128:(b + 1) * 128], cstg[:, b, ch * 128:(ch + 1) * 128], ident)
                    return i_
                P.op("pe", trc, reads=["cstg", "cst", "big"], writes=[ptk])
                copy_op("act", kctx[:, ch, :], pt_[:, 0:256], [ptk, "big"], ["kctx"])
            P.op("sp", lambda e: e.dma_start(out=tabr[0:60, 0:31], in_=nat_bias[l].rearrange("h d m -> (h d) m")),
                 reads=["big"], writes=["tabr"], dsem="tabr")
            pt_, ptk = bank("aux")
            P.op("pe", lambda e, pt_=pt_: e.transpose(pt_[0:31, 0:60], tabr[0:60, 0:31], ident[0:60, 0:60]),
                 reads=["tabr", "cst", "big"], writes=[ptk])
            copy_op("dve", tabT[0:31, 0:60], pt_[0:31, 0:60], [ptk, "big"], ["tabT"])
            tab3 = tabT[0:31, 0:60].rearrange("p (h d) -> p h d", h=4)
            tabA = W(6912, 64)
            tabB = W(6976, 64)
            P.op("dve", lambda e: e.tensor_copy(out=tabA[0:31, 0:56].rearrange("p (h i) -> p h i", h=4), in_=tab3[:, :, 0:14]),
                 reads=["tabT", "big"], writes=["tabA"])
            P.op("dve", lambda e: e.tensor_copy(out=tabB[0:31, 0:56].rearrange("p (h i) -> p h i", h=4), in_=tab3[:, :, 1:15]),
                 reads=["tabT", "big"], writes=["tabA"])
            hb_ = C("hA")[0:31, 0:128]
            for cg in range(8):
                pbk, pk = bank("aux")

                def mkb(e, pbk=pbk, cg=cg):
                    for ci in range(8):
                        c = cg * 8 + ci
                        e.matmul(pbk[0:64, ci * 56:(ci + 1) * 56], hb_[:, 63 - c:127 - c], tabA[0:31, 0:56], start=True, stop=True,
                                 tile_position=(0, 0))
                        i_ = e.matmul(pbk[64:128, ci * 56:(ci + 1) * 56], hb_[:, 63 - c:127 - c], tabB[0:31, 0:56], start=True, stop=True,
                                      tile_position=(0, 64))
                    return i_
                P.op("pe", mkb, reads=["tabA", "cst", "big"], writes=[pk])
                P.op("dve", lambda e, pbk=pbk, cg=cg: e.tensor_tensor(
                    out=bias2[:, :, :, cg * 8:(cg + 1) * 8].rearrange("p h i c -> p c h i"),
                    in0=pbk[:, 0:448].rearrange("p (c h i) -> p c h i", c=8, h=4),
                    in1=C("maskC")[:, cg * 8:(cg + 1) * 8].unsqueeze(2).unsqueeze(3).to_broadcast([128, 8, 4, 14]), op=ALU.add),
                    reads=[pk, "cst", "big"], writes=["bias2"])
            for qc in range(2):
                for tb in range(2):
                    tbs = slice(tb * 512, (tb + 1) * 512)
                    po, pok = bank("mm")
                    pd, pdk = bank("mm")
                    items = [(rr_, hp) for rr_ in range(8) for hp in range(2)]
                    state = {}

                    def emit_sc(i, tb=tb, qc=qc):
                        rr_, hp = items[i]
                        r = tb * 8 + rr_
                        rs_ = min(max(r - 4, 0), 8)
                        dr0 = rs_ - r + 7
                        h = 2 * qc + hp
                        ps, psk = bank("aux")
                        qv = qT[hp * 64:(hp + 1) * 64, qc, r * 64:(r + 1) * 64]

                        def sc(e):
                            for jj in range(4):
                                k0 = (rs_ + 2 * jj) * 64
                                e.matmul(ps[:, jj * 64:(jj + 1) * 64], kT[hp * 64:(hp + 1) * 64, qc, k0:k0 + 128], qv, start=True, stop=True)
                            for b in range(2):
                                i_ = e.matmul(ps[:, 256 + b * 64:256 + (b + 1) * 64], kctx[hp * 64:(hp + 1) * 64, qc, b * 128:(b + 1) * 128], qv,
                                              start=True, stop=True)
                            return i_
                        P.op("pe", sc, reads=["mq", "mk", "kctx", "big"], writes=[psk])
                        tmp, tmpk = tget()
                        P.op("dve", lambda e: e.scalar_tensor_tensor(
                            out=tmp[:, 0:256].rearrange("p (j c) -> p j c", j=4), in0=ps[:, 0:256].rearrange("p (j c) -> p j c", j=4),
                            scalar=0.125, in1=bias2[:, h, dr0:dr0 + 7:2, :], op0=ALU.mult, op1=ALU.add),
                            reads=[psk, "bias2", "big"], writes=[tmpk])
                        pT, pTk = ptget()
                        P.op("act", lambda e: e.activation(out=pT[:, 256:384], in_=ps[:, 256:384], func=AF.Exp, scale=0.125),
                             reads=[psk, "big"], writes=[pTk])
                        P.op("act", lambda e: e.activation(out=pT[:, 0:256], in_=tmp[:, 0:256], func=AF.Exp),
                             reads=[tmpk, "big"], writes=[pTk])
                        state[i] = (pT, pTk, rs_, h)

                    def emit_pv(i, po=po, pd=pd, pok=pok, pdk=pdk):
                        rr_, hp = items[i]
                        pT, pTk, rs_, h = state.pop(i)

                        def pv(e):
                            oc = slice(rr_ * 64, (rr_ + 1) * 64)
                            for j in range(6):
                                if j < 4:
                                    krow = rs_ + 2 * j
                                    vv = vtok[:, krow // 2, h * 64:(h + 1) * 64] if krow % 2 == 0 else vsh[:, (krow - 1) // 2, h * 64:(h + 1) * 64]
                                else:
                                    vv = vctx[:, j - 4, h * 64:(h + 1) * 64]
                                e.matmul(po[hp * 64:(hp + 1) * 64, oc], vv, pT[:, j * 64:(j + 1) * 64], start=(j == 0), stop=(j == 5),
                                         tile_position=(0, hp * 64))
                                i_ = e.matmul(pd[hp * 64:(hp + 1) * 64, oc], CB("ones"), pT[:, j * 64:(j + 1) * 64], start=(j == 0), stop=(j == 5),
                                              tile_position=(0, hp * 64))
                            return i_
                        P.op("pe", pv, reads=[pTk, "mv", "vctx", "cstb", "big"], writes=[pok, pdk])
                    emit_sc(0)
                    for i in range(len(items)):
                        if i + 1 < len(items):
                            emit_sc(i + 1)
                        emit_pv(i)
                    rec, reck = tget()
                    P.op("dve", lambda e, rec=rec, pd=pd: e.reciprocal(out=rec, in_=pd[:]), reads=[pdk, "big"], writes=[reck])
                    P.op("dve", lambda e, rec=rec, po=po, qc=qc, tbs=tbs: e.tensor_tensor(out=yT[:, 4 + qc, tbs], in0=po[:], in1=rec, op=ALU.mult),
                         reads=[pok, reck, "big"], writes=[f"y{4 + qc}"])

        def mixer_ret(g, l):
            isP = g == "P"
            n = 2 if isP else 8
            nseq = 4 if isP else 1
            vk = f"vecs{l}"
            if isP:
                P.op("sp", lambda e: e.dma_start(out=lgt[:], in_=bass.AP(ret_decay.tensor, l * 8, [[0, 128], [1, 8]])),
                     writes=["lgt"], dsem="lgt")
                P.op("act", lambda e: e.activation(out=lgt[:], in_=lgt[:], func=AF.Exp, scale=-1.0), reads=["lgt"], writes=["lgt"])
                P.op("act", lambda e: e.activation(out=lgt[:], in_=lgt[:], func=AF.Ln, bias=1.0), reads=["lgt"], writes=["lgt"])
                P.op("dve", lambda e: e.tensor_scalar(out=lgt[:], in0=lgt[:], scalar1=-1.0, scalar2=None, op0=ALU.mult), reads=["lgt"], writes=["lgt"])
                lg4 = lgt[:].rearrange("p (d q hp) -> p d q hp", d=2, q=2)
                for hp in range(2):
                    P.op("dve", lambda e, hp=hp: e.tensor_copy(out=lgp[hp * 64:(hp + 1) * 64, :, :], in_=lg4[hp * 64:(hp + 1) * 64, :, :, hp]),
                         reads=["lgt"], writes=["lgp"])
                P.op("act", lambda e: e.activation(out=gcht[:], in_=lgp[:], func=AF.Exp, scale=128.0), reads=["lgp"], writes=["gcht"])
                for d in range(2):
                    for h in range(4):
                        P.op("act", lambda e, d=d, h=h: e.activation(out=ztt[:, d * 4 + h:d * 4 + h + 1], in_=C("zf" if d == 0 else "zb"),
                                                                     func=AF.Exp, scale=lgt[:, d * 4 + h:d * 4 + h + 1]),
                             reads=["lgt", "cst"], writes=["ztt"])
                P.op("dve", lambda e: e.tensor_scalar(out=ztt[:], in0=ztt[:], scalar1=0.125, scalar2=None, op0=ALU.mult), reads=["ztt"], writes=["ztt"])
            s1, s1k = load_w_in(l, 1792, 512)
            s2, s2k = load_w_in(l, 2304, 256)
            qT = WB(0, 2048).rearrange("p (c t) -> p c t", c=2)
            kT = WB(1024, 2048).rearrange("p (c t) -> p c t", c=2)
            vtok = WB(2048, 2048).rearrange("p (b c) -> p b c", b=8)
            dmc = W(3072, 512).rearrange("p (h q) -> p h q", h=4)
            xit = W(3584, 256).rearrange("p (d q) -> p d q", d=2)
            qx = WB(3840, 2048).rearrange("p (d t) -> p d t", d=2)
            kz = WB(4864, 2048).rearrange("p (d b c) -> p d b c", d=2, b=8)
            Sb = WB(6144, 2048).rearrange("p (d b h e) -> p d b h e", d=2, b=8, h=2)
            for c in range(2):
                for tb in range(2):
                    tbs = slice(tb * 512, (tb + 1) * 512)
                    pbk, pk = proj_fm(s1, s1k, c * 128, tb)
                    copy_op("act", qT[:, c, tbs], pbk[:], [pk, "big"], ["mq"])
                    pbk, pk = proj_fm(s1, s1k, 256 + c * 128, tb)
                    copy_op("dve", kT[:, c, tbs], pbk[:], [pk, "big"], ["mk"])
            for blk in range(8):
                pbk, pk = proj_tm(s2, s2k, 0, 256, blk * 128)
                copy_op(ew_engine(), vtok[:, blk, :], pbk[:, 0:256], [pk, "big"], ["mv"])
            for h in range(4):
                t0, t0k = tget()
                P.op("act", lambda e, t0=t0, h=h: e.activation(out=t0[:, 0:128], in_=C("diffF"), func=AF.Exp, scale=lgt[:, h:h + 1]),
                     reads=["lgt", "cst", "big"], writes=[t0k])
                P.op("act", lambda e, t0=t0, h=h: e.activation(out=t0[:, 128:256], in_=C("diffB"), func=AF.Exp, scale=lgt[:, 4 + h:5 + h]),
                     reads=["lgt", "cst", "big"], writes=[t0k])
                P.op("dve", lambda e, t0=t0: e.tensor_tensor(out=t0[:, 0:128], in0=t0[:, 0:128], in1=C("maskF"), op=ALU.mult),
                     reads=[t0k, "cst", "big"], writes=[t0k])
                P.op("dve", lambda e, t0=t0: e.tensor_tensor(out=t0[:, 128:256], in0=t0[:, 128:256], in1=C("maskB"), op=ALU.mult),
                     reads=[t0k, "cst", "big"], writes=[t0k])
                P.op("dve", lambda e, t0=t0, h=h: e.tensor_tensor(out=dmc[:, h, :], in0=t0[:, 0:128], in1=t0[:, 128:256], op=ALU.add),
                     reads=[t0k, "big"], writes=["dmc"])
            for qc in range(2):
                for d in range(2):
                    P.op("act", lambda e, d=d, qc=qc: e.activation(out=xit[:, d, :], in_=C("xif" if d == 0 else "xib"), func=AF.Exp,
                                                                   scale=lgp[:, d, qc:qc + 1]), reads=["lgp", "cst", "big"], writes=["xit"])
                    P.op("dve", lambda e, d=d, qc=qc: e.tensor_tensor(
                        out=qx[:, d, :].rearrange("p (b q) -> p b q", b=8), in0=qT[:, qc, :].rearrange("p (b q) -> p b q", b=8),
                        in1=xit[:, d, :].unsqueeze(1).to_broadcast([128, 8, 128]), op=ALU.mult), reads=["xit", "mq", "big"], writes=["qx"])
                for blk in range(8):
                    pbk, pk = proj_tm(s1, s1k, 256 + qc * 128, 128, blk * 128)
                    for d in range(2):
                        P.op("dve", lambda e, pbk=pbk, d=d, blk=blk, qc=qc: e.tensor_tensor(
                            out=kz[:, d, blk, :].rearrange("p (h f) -> p h f", h=2), in0=pbk[:, 0:128].rearrange("p (h f) -> p h f", h=2),
                            in1=ztt[:, d * 4 + 2 * qc:d * 4 + 2 * qc + 2].unsqueeze(2).to_broadcast([128, 2, 64]), op=ALU.mult),
                            reads=[pk, "ztt", "big"], writes=["kz"])
                P.op("dve", lambda e: e.memset(Sb, 0.0), reads=["big"], writes=["Sb"])
                for s in range(nseq):
                    for d in range(2):
                        if isP:
                            P.op("dve", lambda e, d=d: e.memset(sfm[:, d, :], 0.0), writes=[f"sfm{d}"])
                        else:
                            P.op("sp", lambda e, d=d, qc=qc: e.dma_start(
                                out=sfm[:, d, :], in_=sret[l, d, 2 * qc:2 * qc + 2].rearrange("h p e -> (h p) e")),
                                writes=[f"sfm{d}"], dsem=f"sfm{d}")
                    for step in range(n):
                        for d in range(2):
                            i = step if d == 0 else n - 1 - step
                            blk = s * n + i
                            for hp in range(2):
                                copy_op("act", Sb[hp * 64:(hp + 1) * 64, d, blk, hp, :], sfm[hp * 64:(hp + 1) * 64, d, :],
                                        [f"sfm{d}", "big"], ["Sb"])
                            ps, psk = bank("aux")

                            def su(e, ps=ps, d=d, blk=blk, qc=qc):
                                for hp in range(2):
                                    i_ = e.matmul(ps[hp * 64:(hp + 1) * 64, 0:64], kz[:, d, blk, hp * 64:(hp + 1) * 64],
                                                  vtok[:, blk, (2 * qc + hp) * 64:(2 * qc + hp + 1) * 64], start=True, stop=True,
                                                  tile_position=(0, hp * 64))
                                return i_
                            P.op("pe", su, reads=["kz", "mv", "big"], writes=[psk])
                            P.op("dve", lambda e, ps=ps, d=d, qc=qc: e.scalar_tensor_tensor(
                                out=sfm[:, d, :], in0=sfm[:, d, :], scalar=gcht[:, d, qc:qc + 1], in1=ps[:, 0:64], op0=ALU.mult, op1=ALU.add),
                                reads=[psk, f"sfm{d}", "gcht"], writes=[f"sfm{d}"])
                    if isP:
                        for d in range(2):
                            P.op("sp", lambda e, d=d, s=s, qc=qc: e.dma_start(
                                out=nret[s, l, d, 2 * qc:2 * qc + 2].rearrange("h p e -> (h p) e"), in_=sfm[:, d, :]),
                                reads=[f"sfm{d}"], dsem=f"nret{d}")
                sg_, sgk = load_w_in(l, 2560 + qc * 128, 128)
                pos = []
                for tb in range(2):
                    tbs = slice(tb * 512, (tb + 1) * 512)
                    po, pok = bank("mm")
                    pos.append((po, pok))
                    items = [(b4, hp) for b4 in range(4) for hp in range(2)]
                    state = {}

                    def emit_sc(i, tb=tb, qc=qc):
                        b4, hp = items[i]
                        blk = tb * 4 + b4
                        bs = slice(blk * 128, (blk + 1) * 128)
                        h = 2 * qc + hp
                        ps, psk = bank("aux")
                        P.op("pe", lambda e: e.matmul(
                            ps[:, 0:128], kT[hp * 64:(hp + 1) * 64, qc, bs], qT[hp * 64:(hp + 1) * 64, qc, bs], start=True, stop=True),
                            reads=["mq", "mk", "big"], writes=[psk])
                        smi = rr["pt"] % 4
                        rr["pt"] += 1
                        sm, smk = WB(5888 + smi * 64, 128), f"sm{smi}"
                        P.op("dve", lambda e: e.tensor_tensor(out=sm[:, 0:128], in0=ps[:, 0:128], in1=dmc[:, h, :], op=ALU.mult),
                             reads=[psk, "dmc", "big"], writes=[smk])
                        state[i] = (sm, smk)

                    def emit_om(i, po=po, pok=pok, tb=tb, qc=qc):
                        b4, hp = items[i]
                        blk = tb * 4 + b4
                        bs = slice(blk * 128, (blk + 1) * 128)
                        h = 2 * qc + hp
                        sm, smk = state.pop(i)

                        def om(e):
                            oc = slice(b4 * 128, (b4 + 1) * 128)
                            e.matmul(po[hp * 64:(hp + 1) * 64, oc], vtok[:, blk, h * 64:(h + 1) * 64], sm[:, 0:128], start=True, stop=False,
                                     tile_position=(0, hp * 64))
                            e.matmul(po[hp * 64:(hp + 1) * 64, oc], Sb[:, 0, blk, hp, :], qx[:, 0, bs],
                                     start=False, stop=False, tile_position=(0, hp * 64))
                            return e.matmul(po[hp * 64:(hp + 1) * 64, oc], Sb[:, 1, blk, hp, :], qx[:, 1, bs],
                                            start=False, stop=True, tile_position=(0, hp * 64))
                        P.op("pe", om, reads=[smk, "mv", "Sb", "qx", "big"], writes=[pok])
                    emit_sc(0)
                    for i in range(len(items)):
                        if i + 1 < len(items):
                            emit_sc(i + 1)
                        emit_om(i)
                for tb in range(2):
                    tbs = slice(tb * 512, (tb + 1) * 512)
                    po, pok = pos[tb]
                    o_, ok_ = tget()
                    sq_, sqk_ = tget()
                    P.op("act", lambda e, o_=o_, po=po: e.activation(out=o_, in_=po[:], func=AF.Copy), reads=[pok, "big"], writes=[ok_])
                    P.op("act", lambda e, sq_=sq_, po=po: e.activation(out=sq_, in_=po[:], func=AF.Square), reads=[pok, "big"], writes=[sqk_])
                    pmu, pmuk = bank("aux")
                    pms, pmsk = bank("aux")
                    P.op("pe", lambda e, pmu=pmu, o_=o_: e.matmul(pmu[:], C("b64"), o_, start=True, stop=True), reads=[ok_, "cst", "big"], writes=[pmuk])
                    P.op("pe", lambda e, pms=pms, sq_=sq_: e.matmul(pms[:], C("b64"), sq_, start=True, stop=True), reads=[sqk_, "cst", "big"], writes=[pmsk])
                    P.op("dve", lambda e, o_=o_, pmu=pmu: e.tensor_tensor(out=o_, in0=o_, in1=pmu[:], op=ALU.subtract), reads=[ok_, pmuk, "big"], writes=[ok_])
                    P.op("act", lambda e, sq_=sq_, pmu=pmu: e.activation(out=sq_, in_=pmu[:], func=AF.Square), reads=[pmuk, "big"], writes=[sqk_])
                    P.op("dve", lambda e, sq_=sq_, pms=pms: e.tensor_tensor(out=sq_, in0=pms[:], in1=sq_, op=ALU.subtract), reads=[sqk_, pmsk, "big"], writes=[sqk_])
                    P.op("act", lambda e, sq_=sq_: e.activation(out=sq_, in_=sq_, func=AF.Ln, bias=EPS), reads=[sqk_, "big"], writes=[sqk_])
                    P.op("act", lambda e, sq_=sq_: e.activation(out=sq_, in_=sq_, func=AF.Exp, scale=-0.5), reads=[sqk_, "big"], writes=[sqk_])
                    P.op("dve", lambda e, o_=o_, sq_=sq_, qc=qc: e.scalar_tensor_tensor(
                        out=o_, in0=o_, scalar=vecs[:, l, 118 + qc:119 + qc], in1=sq_, op0=ALU.mult, op1=ALU.mult),
                        reads=[ok_, sqk_, vk, "big"], writes=[ok_])
                    pg, pgk = proj_fm(sg_, sgk, 0, tb)
                    P.op("act", lambda e, sq_=sq_, pg=pg: e.activation(out=sq_, in_=pg[:], func=AF.Silu), reads=[pgk, "big"], writes=[sqk_])
                    P.op("dve", lambda e, o_=o_, sq_=sq_, qc=qc, tbs=tbs: e.tensor_tensor(out=yT[:, 6 + qc, tbs], in0=o_, in1=sq_, op=ALU.mult),
                         reads=[ok_, sqk_, "big"], writes=[f"y{6 + qc}"])

        def wout_proj(g, l):
            v = 1 if g == "P" else 0
            for nh in range(2):
                sl, slk = load_slot([((lambda sl, k=k: sl[:, k:k + 4, :]),
                                      w_out[l, k * 128:(k + 4) * 128, nh * 512:(nh + 1) * 512].rearrange("(c p) n -> p c n", p=128)) for k in (0, 4)])
                for nn in range(4):
                    n = nh * 4 + nn
                    for tb in range(2):
                        tbs = slice(tb * 512, (tb + 1) * 512)
                        po, pok = bank("mm")

                        def mmo(e, po=po, sl=sl, nn=nn, tbs=tbs):
                            for k in range(8):
                                i_ = e.matmul(po[:], sl[:, k, nn * 128:(nn + 1) * 128], yT[:, k, tbs], start=(k == 0), stop=(k == 7))
                            return i_
                        P.op("pe", mmo, reads=[slk, "big"] + [f"y{k}" for k in range(8)], writes=[pok])
                        P.op("dve", lambda e, po=po, n=n, tbs=tbs: e.scalar_tensor_tensor(
                            out=xres[g][:, n, tbs], in0=po[:], scalar=coefG[:, l, v, 1, n:n + 1], in1=xres[g][:, n, tbs],
                            op0=ALU.mult, op1=ALU.add), reads=[pok, f"coef{l}", xk(g, n, tb)], writes=[xk(g, n, tb)])

        def mixer(g, l, mix):
            import os
            if g not in os.environ.get("KDBG_GROUPS", "PS"):
                mix = ""
            norm(g, l, 1)
            for c in range(8):
                P.op("dve", lambda e, c=c: e.memset(yT[:, c, :], 0.0), reads=["big"], writes=[f"y{c}"])
            if "A" in mix:
                mixer_A(g, l)
                barrier()
            if "B" in mix:
                mixer_attn(g, l, "B")
                P.muted = False
                barrier()
            if "C" in mix:
                if g == "P":
                    mixer_attn(g, l, "C")
                else:
                    mixer_nat(l)
                P.muted = False
                barrier()
            if "D" in mix:
                mixer_ret(g, l)
                barrier()
            wout_proj(g, l)
            barrier()

        import os as _os
        for l in range(depth):
            for g in ("P", "S"):
                if _os.environ.get("KDBG_SKIPFFN"):
                    break
                norm(g, l, 0)
                ffn(g, l, 0, 0)
            barrier()
            if stop == "ffn1":
                break
            for g in ("P", "S"):
                mixer(g, l, mix)
            if stop == "mix":
                break
            for g in ("P", "S"):
                norm(g, l, 2)
                ffn(g, l, 1, 2)

        barrier()
        for g in ("P", "S"):
            for blk in range(8):
                stg = stage[blk % 2]
                sk = f"stg{blk % 2}"
                for half in range(2):
                    pbk, pk = bank("aux")

                    def tr2(e, pbk=pbk, half=half, blk=blk, g=g):
                        for c4 in range(4):
                            i_ = e.transpose(pbk[:, c4 * 128:(c4 + 1) * 128],
                                             xres[g][:, half * 4 + c4, blk * 128:(blk + 1) * 128], ident)
                        return i_
                    P.op("pe", tr2, reads=["cst"] + [xk(g, c, blk // 4) for c in range(half * 4, half * 4 + 4)], writes=[pk])
                    copy_op(ew_engine(), stg[:, half * 512:(half + 1) * 512], pbk[:], [pk, "big"], [sk])
                P.op("sp", (lambda stg, dst: lambda e: e.dma_start(out=dst, in_=stg))(stg, yout[g][blk * 128:(blk + 1) * 128, :]),
                     reads=[sk], dsem="yst" + sk)
        pbk, pk = bank("aux")
        P.op("pe", lambda e, pbk=pbk: e.transpose(pbk[0:64, 0:128], lruo[:], ident), reads=["lruo", "cst"], writes=[pk])
        copy_op("dve", rows[0:64, :], pbk[0:64, 0:128], [pk], ["rows"])
        P.op("sp", lambda e: e.dma_start(out=nlru.rearrange("s l d (c p) -> (s l d c) p", p=128), in_=rows[0:64, :]),
             reads=["rows"], dsem="nlru")
        P.emit(st)
        nc._prog_stats = P.stats
    return nc


_NC_CACHE = {}


def _get_nc(depth=DEPTH, stop=None, mix="ABCD"):
    key = (depth, stop, mix)
    if key not in _NC_CACHE:
        _NC_CACHE[key] = build_program(depth, stop, mix)
    return _NC_CACHE[key]


def kernel(x_prompt, x_sample, cache_b_k, cache_b_v, cache_c_k, cache_c_v, state_lru, state_ret,
           c, c_ctx, w_mod, b_mod, norm_g, ffn_w_in, ffn_w_out, w_in, w_out, conv_w, conv_b,
           lru_w_r, lru_b_r, lru_w_i, lru_b_i, lru_lambda, gqa_qn, gqa_kn, nat_qn, nat_kn,
           nat_bias, ret_decay, ret_gn, _depth=DEPTH, _stop=None, _mix="ABCD", _ncores=8):
    f = lambda a: np.ascontiguousarray(np.asarray(a, dtype=np.float32))
    x_prompt, x_sample = f(x_prompt), f(x_sample)
    shared = dict(w_mod=f(w_mod), b_mod=f(b_mod), norm_g=f(norm_g), ffn_w_in=f(ffn_w_in), ffn_w_out=f(ffn_w_out),
                  w_in=f(w_in), w_out=f(w_out), conv_w=f(conv_w), conv_b=f(conv_b), lru_w_r=f(lru_w_r),
                  lru_b_r=f(lru_b_r), lru_w_i=f(lru_w_i), lru_b_i=f(lru_b_i), lru_lambda=f(lru_lambda),
                  gqa_qn=f(gqa_qn), gqa_kn=f(gqa_kn), nat_qn=f(nat_qn), nat_kn=f(nat_kn), nat_bias=f(nat_bias),
                  ret_decay=f(ret_decay), ret_gn=f(ret_gn), cst=_CST, cstb=_CSTB)
    c = f(c)
    c_ctx = f(c_ctx)
    cache_b_k, cache_b_v, cache_c_k, cache_c_v = f(cache_b_k), f(cache_b_v), f(cache_c_k), f(cache_c_v)
    state_lru, state_ret = f(state_lru), f(state_ret)
    in_maps = []
    for i in range(_ncores):
        b = i // 2
        m = dict(shared)
        m["xp"] = x_prompt[4 * i:4 * i + 4].reshape(T, D)
        m["xs"] = x_sample[b]
        m["cvec"] = np.stack([c[b], c_ctx], 0)
        m["cbk"] = cache_b_k[b].reshape(DEPTH, 256, 128)
        m["cbv"] = cache_b_v[b].reshape(DEPTH, 256, 128)
        m["cck"] = cache_c_k[b].reshape(DEPTH, 256, 256)
        m["ccv"] = cache_c_v[b].reshape(DEPTH, 256, 256)
        m["slru"] = state_lru[b]
        m["sret"] = state_ret[b]
        in_maps.append(m)
    nc = _get_nc(_depth, _stop, _mix)
    res = run_bass_kernel_spmd(nc, in_maps, core_ids=list(range(_ncores)))
    r = res.results
    n = _ncores
    y_prompt = np.concatenate([r[i]["yp"].reshape(4, 256, D) for i in range(n)], 0)
    y_sample = np.stack([r[2 * b]["ys"] for b in range(n // 2)], 0)
    nbk = np.concatenate([r[i]["nbk"].reshape(4, DEPTH, 256, 2, 64) for i in range(n)], 0)
    nbv = np.concatenate([r[i]["nbv"].reshape(4, DEPTH, 256, 2, 64) for i in range(n)], 0)
    nck = np.concatenate([r[i]["nck"].reshape(4, DEPTH, 256, 4, 64) for i in range(n)], 0)
    ncv = np.concatenate([r[i]["ncv"].reshape(4, DEPTH, 256, 4, 64) for i in range(n)], 0)
    nlru = np.concatenate([r[i]["nlru"] for i in range(n)], 0)
    nret = np.concatenate([r[i]["nret"] for i in range(n)], 0)
    return y_prompt, y_sample, nbk, nbv, nck, ncv, nlru, nret
```

```python
import numpy as np
from contextlib import ExitStack
import concourse.bass as bass
import concourse.mybir as mybir
from concourse.bass_utils import run_bass_kernel_spmd

F32 = mybir.dt.float32
BF16 = mybir.dt.bfloat16
AF = mybir.ActivationFunctionType
ALU = mybir.AluOpType

D = 1024
DEPTH = 4
T = 1024
DFF = 2816
INC = 2816
EPS = 1e-6
ENGS = ("pe", "act", "dve", "pool", "sp")


class Prog:
    def __init__(self, nc, same_engine_sync=True):
        self.nc = nc
        self.ops = []
        self.lastw = {}
        self.readers = {}
        self.same_engine_sync = same_engine_sync
        self.pw = {}
        self.old_skip = True

    muted = False

    def op(self, eng, fn, reads=(), writes=(), dsem=None, ninc=1):
        if self.muted:
            return -1
        pr = [k for k in reads if k.startswith("pb")]
        if pr:
            reads = [k for k in reads if not k.startswith("pb")]
            writes = list(writes) + pr
        i = len(self.ops)
        deps = set()
        raw = set()
        for k in reads:
            w = self.lastw.get(k)
            if w is not None:
                deps.add(w)
                raw.add(w)
        for k in writes:
            w = self.lastw.get(k)
            if w is not None:
                deps.add(w)
                if k in pr:
                    pass
            rs = self.readers.get(k)
            if rs:
                deps.update(rs)
        for k in reads:
            self.readers.setdefault(k, []).append(i)
        for k in writes:
            self.lastw[k] = i
            self.readers[k] = []
        deps.discard(i)
        for k in pr:
            w = self.pw.get(k)
            if w is not None:
                raw.add(w)
        for k in writes:
            if k.startswith("pb") and k not in pr:
                self.pw[k] = i
        raw.discard(i)
        self.ops.append(dict(eng=eng, fn=fn, deps=deps, raw=raw, dsem=dsem, ninc=ninc, idx=i))
        return i

    def _skip(self, p, o):
        if not (p["dsem"] is None and o["dsem"] is None and p["eng"] == o["eng"]):
            return False
        if p["eng"] == "pe" or not self.same_engine_sync:
            return True
        if self.old_skip:
            return False
        return p["idx"] not in o["raw"]

    def emit(self, stack):
        nc = self.nc
        ops = self.ops
        need = [False] * len(ops)
        for o in ops:
            for d in o["deps"]:
                if not self._skip(ops[d], o):
                    need[d] = True
        cnt = {e: 0 for e in ENGS}
        dcnt = {}
        for i, o in enumerate(ops):
            if o["dsem"] is not None:
                dcnt[o["dsem"]] = dcnt.get(o["dsem"], 0) + 16 * o["ninc"]
                o["sig"] = (("d", o["dsem"]), dcnt[o["dsem"]])
            elif need[i]:
                cnt[o["eng"]] += 1
                o["sig"] = (("e", o["eng"]), cnt[o["eng"]])
            else:
                o["sig"] = None
        assert max(cnt.values()) < 60000, cnt
        assert max(dcnt.values()) < 60000, dcnt
        sems = {}
        for e in ENGS:
            sems[("e", e)] = stack.enter_context(nc.semaphore("s_" + e))
        for d in dcnt:
            sems[("d", d)] = stack.enter_context(nc.semaphore("d_" + d))
        self.stats = dict(cnt=cnt, nsem=len(sems), nops=len(ops))
        block = stack.enter_context(nc.Block())
        dfinal = dict(dcnt)

        def run(ename):
            def body(eng):
                waited = {}
                for o in ops:
                    if o["eng"] != ename:
                        continue
                    wl = {}
                    for d in o["deps"]:
                        p = ops[d]
                        if p["sig"] is None or self._skip(p, o):
                            continue
                        s, v = p["sig"]
                        if wl.get(s, 0) < v:
                            wl[s] = v
                    for s, v in wl.items():
                        if waited.get(s, 0) < v:
                            eng.wait_ge(sems[s], v)
                            waited[s] = v
                    ins = o["fn"](eng)
                    if o["sig"] is not None:
                        s, v = o["sig"]
                        if s[0] == "d":
                            for i_ in (ins if isinstance(ins, (list, tuple)) else [ins]):
                                i_.then_inc(sems[s], 16)
                        else:
                            ins.then_inc(sems[s], 1)
                if ename == "sp":
                    for d, v in dfinal.items():
                        eng.wait_ge(sems[("d", d)], v)
            return body

        block.tensor(run("pe"))
        block.scalar(run("act"))
        block.vector(run("dve"))
        block.gpsimd(run("pool"))
        block.sync(run("sp"))


def _const_tables():
    cols = {}
    parts = []
    off = [0]

    def add(name, arr):
        a = np.zeros((128, arr.shape[1]), np.float32)
        a[:arr.shape[0]] = arr
        cols[name] = (off[0], arr.shape[1])
        off[0] += arr.shape[1]
        parts.append(a)

    add("ident", np.eye(128, dtype=np.float32))
    add("J", np.eye(128, dtype=np.float32)[::-1].copy())
    b64 = np.zeros((128, 128), np.float32)
    b64[:64, :64] = 1.0 / 64
    b64[64:, 64:] = 1.0 / 64
    add("b64", b64)
    t = np.arange(T)
    row = (t // 64).astype(np.float32)
    col = (t % 64).astype(np.float32)
    inv = (10000.0 ** (-np.arange(16, dtype=np.float32) / 16)).astype(np.float32)
    ang = np.concatenate([row[:, None] * inv, col[:, None] * inv], -1).astype(np.float32)
    cosT = np.cos(ang).T
    sinT = np.sin(ang).T
    sgn = np.where(np.arange(64) % 2 == 0, -1.0, 1.0)
    CA = np.ones((64, 16), np.float32); SA = np.zeros((64, 16), np.float32)
    CB_ = np.ones((64, 64), np.float32); SB = np.zeros((64, 64), np.float32)
    for d_ in range(64):
        j = d_ // 2
        if j < 16:
            CA[d_] = np.cos(np.arange(16, dtype=np.float32) * inv[j])
            SA[d_] = sgn[d_] * np.sin(np.arange(16, dtype=np.float32) * inv[j])
        else:
            CB_[d_] = np.cos(np.arange(64, dtype=np.float32) * inv[j - 16])
            SB[d_] = sgn[d_] * np.sin(np.arange(64, dtype=np.float32) * inv[j - 16])
    add("CA", np.concatenate([CA, CA], 0))
    add("SA", np.concatenate([SA, SA], 0))
    add("CB", np.concatenate([CB_, CB_], 0))
    add("SB", np.concatenate([SB, SB], 0))
    psw = np.zeros((128, 128), np.float32)
    for k in range(128):
        psw[k, k ^ 1] = 1.0
    add("pswap", psw)
    kk = np.arange(128)[:, None].astype(np.float32)
    qq = np.arange(128)[None, :].astype(np.float32)
    add("diffF", np.maximum(qq - kk, 0.0))
    add("maskF", (qq >= kk).astype(np.float32) / 8.0)
    add("diffB", np.maximum(kk - qq, 0.0))
    add("maskB", (kk >= qq).astype(np.float32) / 8.0)
    add("xif", np.broadcast_to(qq + 1.0, (128, 128)).copy())
    add("xib", np.broadcast_to(128.0 - qq, (128, 128)).copy())
    add("zf", 127.0 - kk)
    add("zb", kk.copy())
    hb = np.zeros((31, 127), np.float32)
    for m in range(31):
        hb[m, m + 48] = 1.0
    hA = np.zeros((31, 256), np.float32)
    hA[:, 0:127] = hb
    hB = np.zeros((31, 256), np.float32)
    hB[:, 128:255] = hb
    add("hA", hA)
    add("hB", hB)
    cc = np.arange(64)
    cs = np.clip(cc - 8, 0, 48)
    inw = (cc[None, :] >= cs[:, None]) & (cc[None, :] < cs[:, None] + 16)
    mk = np.where(inw.T, 0.0, -30000.0).astype(np.float32)
    add("maskC", np.concatenate([mk, mk], 0))
    cst = np.concatenate(parts, 1)
    cb = {}
    pb = []
    ob = [0]

    def addb(name, arr):
        cb[name] = (ob[0], arr.shape[1])
        ob[0] += arr.shape[1]
        pb.append(arr.astype(np.float32))

    addb("ones1024", np.full((128, 128), 1.0 / 1024, np.float32))
    addb("ones", np.ones((128, 64), np.float32))
    cstb = np.concatenate(pb, 1)
    return cst, cols, cstb, cb


_CST, _CCOL, _CSTB, _CBCOL = _const_tables()


def build_program(depth=DEPTH, stop=None, mix="ABCD", same_engine_sync=True):
    nc = bass.Bass("TRN2", target_bir_lowering=False)

    def din(name, shape):
        return nc.dram_tensor(name, list(shape), F32, kind="ExternalInput").ap()

    def dout(name, shape):
        return nc.dram_tensor(name, list(shape), F32, kind="ExternalOutput").ap()

    xin = {"P": din("xp", [T, D]), "S": din("xs", [T, D])}
    cvec = din("cvec", [2, D])
    w_mod = din("w_mod", [DEPTH, D, 9 * D])
    b_mod = din("b_mod", [DEPTH, 9 * D])
    norm_g = din("norm_g", [DEPTH, 3, D])
    ffn_w_in = din("ffn_w_in", [DEPTH, 2, D, 2 * DFF])
    ffn_w_out = din("ffn_w_out", [DEPTH, 2, DFF, D])
    w_in = din("w_in", [DEPTH, D, INC])
    w_out = din("w_out", [DEPTH, D, D])
    cbk = din("cbk", [DEPTH, 256, 128]); cbv = din("cbv", [DEPTH, 256, 128])
    cck = din("cck", [DEPTH, 256, 256]); ccv = din("ccv", [DEPTH, 256, 256])
    slru = din("slru", [DEPTH, 2, 256]); sret = din("sret", [DEPTH, 2, 4, 64, 64])
    conv_w = din("conv_w", [DEPTH, 4, 256]); conv_b = din("conv_b", [DEPTH, 256])
    lru_w_r = din("lru_w_r", [DEPTH, 2, 4, 64, 64]); lru_b_r = din("lru_b_r", [DEPTH, 2, 256])
    lru_w_i = din("lru_w_i", [DEPTH, 2, 4, 64, 64]); lru_b_i = din("lru_b_i", [DEPTH, 2, 256])
    lru_lambda = din("lru_lambda", [DEPTH, 2, 256])
    gqa_qn = din("gqa_qn", [DEPTH, 64]); gqa_kn = din("gqa_kn", [DEPTH, 64])
    nat_qn = din("nat_qn", [DEPTH, 64]); nat_kn = din("nat_kn", [DEPTH, 64])
    nat_bias = din("nat_bias", [DEPTH, 4, 15, 31]); ret_decay = din("ret_decay", [DEPTH, 2, 4])
    ret_gn = din("ret_gn", [DEPTH, 256])
    nbk = dout("nbk", [4, DEPTH, 256, 128]); nbv = dout("nbv", [4, DEPTH, 256, 128])
    nck = dout("nck", [4, DEPTH, 256, 256]); ncv = dout("ncv", [4, DEPTH, 256, 256])
    nlru = dout("nlru", [4, DEPTH, 2, 256]); nret = dout("nret", [4, DEPTH, 2, 4, 64, 64])
    cst = din("cst", list(_CST.shape))
    cstb = din("cstb", list(_CSTB.shape))
    yout = {"P": dout("yp", [T, D]), "S": dout("ys", [T, D])}

    with ExitStack() as st:
        def sb(name, shape, dt=F32):
            return st.enter_context(nc.sbuf_tensor(name, list(shape), dt))

        P = Prog(nc, same_engine_sync=same_engine_sync)
        xres = {"P": sb("xP", [128, 8, T]), "S": sb("xS", [128, 8, T])}
        hbf = sb("hbf", [128, 8, T], BF16)
        BIGW = 15360
        big = sb("big", [128, BIGW])
        bigb = big[:].bitcast(BF16)
        slots = [sb(f"slot{i}", [128, 8, 512], BF16) for i in range(4)]
        cstt = sb("cstt", list(_CST.shape))
        cstbt = sb("cstbt", list(_CSTB.shape), BF16)
        vecs = sb("vecs", [128, DEPTH, 128])
        modc = sb("modc", [128, DEPTH, 72, 2])
        coefA = sb("coefA", [128, DEPTH, 2, 3, 8])
        coefG = sb("coefG", [128, DEPTH, 2, 3, 8])
        rows = sb("rows", [128, 128])
        crow = sb("crow", [16, 128])
        scT = sb("scT", [128, 16], BF16)
        sqb = [sb(f"sq{i}", [128, 512], BF16) for i in range(2)]
        mslots = [sb(f"mslot{i}", [128, 8, 128], BF16) for i in range(2)]
        rstdb = [sb(f"rstd{i}", [128, 512]) for i in range(2)]
        tmpf = [sb(f"tmpf{i}", [128, 512]) for i in range(3)]
        dummy = sb("dmy_t", [128, 2])
        lruc = sb("lruc", [128, DEPTH, 2, 4])
        lrut = sb("lrut", [128, 4])
        psb = [st.enter_context(nc.psum_tensor(f"pb{i}", [128, 512], F32)) for i in range(8)]

        def C(name):
            o, w = _CCOL[name]
            return cstt[:, o:o + w]

        def CB(name):
            o, w = _CBCOL[name]
            return cstbt[:, o:o + w]

        rr = {"mm": 0, "aux": 0, "slot": 0, "ev": 0, "sq": 0, "tmp": 0}

        def bank(cls):
            if cls == "mm":
                i = rr["mm"] % 4
                rr["mm"] += 1
            else:
                i = 4 + rr["aux"] % 4
                rr["aux"] += 1
            return psb[i], f"pb{i}"

        def ew_engine():
            rr["ev"] += 1
            return "act" if rr["ev"] % 2 else "dve"

        def barrier():
            P.op("dve", lambda e: e.memset(dummy[:], 0.0), writes=["big", "dummy"])

        def copy_op(eng, out, in_, reads, writes):
            if eng == "act":
                P.op("act", lambda e: e.activation(out=out, in_=in_, func=AF.Copy), reads=reads, writes=writes)
            else:
                P.op(eng, lambda e: e.tensor_copy(out=out, in_=in_), reads=reads, writes=writes)

        def load_slot(pairs):
            k = rr["slot"] % 4
            rr["slot"] += 1
            sl = slots[k]
            pr = [(f(sl), s) for f, s in pairs]
            P.op("pool", lambda e: [e.dma_start(out=d, in_=s) for d, s in pr],
                 writes=[f"slot{k}"], dsem=f"slot{k}", ninc=len(pr))
            return sl, f"slot{k}"

        def xk(g, c, tb):
            return f"x{g}{c}.{tb}"

        def hk(c, tb):
            return f"h{c}.{tb}"

        P.op("sp", lambda e: e.dma_start(out=cstt[:], in_=cst), writes=["cst"], dsem="cst")
        P.op("pool", lambda e: e.dma_start(out=cstbt[:], in_=cstb), writes=["cstb"], dsem="cstb")
        P.op("dve", lambda e: e.memset(rows[:], 0.0), writes=["rows"])
        ident = C("ident")

        stage = [big[:, 0:1024], big[:, 1024:2048]]
        for g in ("P", "S"):
            for blk in range(8):
                stg = stage[blk % 2]
                sk = f"stg{blk % 2}"
                P.op("sp", (lambda stg, src: lambda e: e.dma_start(out=stg, in_=src))(stg, xin[g][blk * 128:(blk + 1) * 128, :]),
                     reads=["big"], writes=[sk], dsem=sk)
                for half in range(2):
                    pbk, pk = bank("aux")

                    def tr(e, pbk=pbk, stg=stg, half=half):
                        for c4 in range(4):
                            i_ = e.transpose(pbk[:, c4 * 128:(c4 + 1) * 128],
                                             stg[:, (half * 4 + c4) * 128:(half * 4 + c4 + 1) * 128], ident)
                        return i_
                    P.op("pe", tr, reads=[sk, "cst", "big"], writes=[pk])
                    copy_op(ew_engine(), xres[g][:, half * 4:(half + 1) * 4, blk * 128:(blk + 1) * 128],
                            pbk[:].rearrange("p (c t) -> p c t", c=4), [pk],
                            [xk(g, c, blk // 4) for c in range(half * 4, half * 4 + 4)])

        P.op("sp", lambda e: e.dma_start(out=crow[:], in_=cvec.rearrange("v (c p) -> (v c) p", p=128)),
             writes=["crow"], dsem="crow")
        pbk, pk = bank("aux")
        P.op("pe", lambda e, pbk=pbk: e.transpose(pbk[:, 0:16], crow[:], ident[0:16, 0:16]), reads=["crow", "cst"], writes=[pk])
        P.op("act", lambda e, pbk=pbk: e.activation(out=scT[:], in_=pbk[:, 0:16], func=AF.Silu), reads=[pk], writes=["scT"])

        for l in range(depth):
            def ld_rows(e, l=l):
                ins = []
                ins.append(e.dma_start(out=rows[0:72, :], in_=b_mod[l].rearrange("(r p) -> r p", p=128)))
                ins.append(e.dma_start(out=rows[72:96, :], in_=norm_g[l].rearrange("s (c p) -> (s c) p", p=128)))
                ins.append(e.dma_start(out=rows[96:104, :], in_=conv_w[l].rearrange("i (c p) -> (i c) p", p=128)))
                ins.append(e.dma_start(out=rows[104:106, :], in_=conv_b[l].rearrange("(c p) -> c p", p=128)))
                ins.append(e.dma_start(out=rows[106:110, :], in_=lru_b_r[l].rearrange("d (c p) -> (d c) p", p=128)))
                ins.append(e.dma_start(out=rows[110:114, :], in_=lru_b_i[l].rearrange("d (c p) -> (d c) p", p=128)))
                ins.append(e.dma_start(out=rows[114:118, :], in_=lru_lambda[l].rearrange("d (c p) -> (d c) p", p=128)))
                ins.append(e.dma_start(out=rows[118:120, :], in_=ret_gn[l].rearrange("(c p) -> c p", p=128)))
                for i_, v_ in enumerate((gqa_qn, gqa_kn, nat_qn, nat_kn)):
                    ins.append(e.dma_start(out=rows[120 + i_:121 + i_, 0:64], in_=v_[l:l + 1, :]))
                    ins.append(e.dma_start(out=rows[120 + i_:121 + i_, 64:128], in_=v_[l:l + 1, :]))
                ins.append(e.dma_start(out=rows[124:128, :], in_=slru[l].rearrange("d (c p) -> (d c) p", p=128)))
                return ins
            P.op("sp", ld_rows, writes=["rows"], dsem="rows", ninc=17)
            pbk, pk = bank("aux")
            P.op("pe", lambda e, pbk=pbk: e.transpose(pbk[:, 0:128], rows[:], ident), reads=["rows", "cst"], writes=[pk])
            copy_op("dve", vecs[:, l, :], pbk[:, 0:128], [pk], [f"vecs{l}"])
            P.op("act", lambda e, l=l: e.activation(out=lrut[:], in_=vecs[:, l, 114:118], func=AF.Exp, scale=-1.0), reads=[f"vecs{l}"], writes=["lrut"])
            P.op("act", lambda e: e.activation(out=lrut[:], in_=lrut[:], func=AF.Ln, bias=1.0), reads=["lrut"], writes=["lrut"])
            P.op("dve", lambda e, l=l: e.tensor_scalar(out=lruc[:, l, 0, :], in0=lrut[:], scalar1=-8.0, scalar2=None, op0=ALU.mult), reads=["lrut"], writes=["lruc"])
            P.op("dve", lambda e, l=l: e.tensor_scalar(out=lruc[:, l, 1, :], in0=lrut[:], scalar1=-16.0, scalar2=None, op0=ALU.mult), reads=["lrut"], writes=["lruc"])
        def mod_step(l, s):
            mi = rr["ms"] % 2
            rr["ms"] += 1
            sl, sk = mslots[mi], f"mslot{mi}"
            P.op("pool", lambda e: [e.dma_start(out=sl[:, k:k + 4, :],
                                                in_=w_mod[l, k * 128:(k + 4) * 128, s * 128:(s + 1) * 128].rearrange("(c p) n -> p c n", p=128))
                                    for k in (0, 4)], writes=[sk], dsem=sk, ninc=2)
            pm, pmk = bank("aux")

            def mv(e):
                for k in range(8):
                    i_ = e.matmul(pm[:, 0:2], sl[:, k, :], scT[:, k::8], start=(k == 0), stop=(k == 7))
                return i_
            P.op("pe", mv, reads=[sk, "scT"], writes=[pmk])
            P.op("dve", lambda e: e.tensor_tensor(out=modc[:, l, s, :], in0=pm[:, 0:2],
                                                  in1=vecs[:, l, s:s + 1].to_broadcast([128, 2]), op=ALU.add),
                 reads=[pmk, f"vecs{l}"], writes=[f"modc{l}"])
            if s % 24 == 23:
                s3 = s // 24
                for v in range(2):
                    P.op("dve", lambda e, v=v, s3=s3: e.scalar_tensor_tensor(
                        out=coefA[:, l, v, s3, :], in0=modc[:, l, (3 * s3 + 1) * 8:(3 * s3 + 2) * 8, v], scalar=1.0,
                        in1=vecs[:, l, 72 + s3 * 8:72 + s3 * 8 + 8], op0=ALU.add, op1=ALU.mult),
                        reads=[f"modc{l}", f"vecs{l}"], writes=[f"coef{l}"])
                    P.op("dve", lambda e, v=v, s3=s3: e.tensor_scalar(
                        out=coefG[:, l, v, s3, :], in0=modc[:, l, (3 * s3 + 2) * 8:(3 * s3 + 3) * 8, v],
                        scalar1=(1.0 if s3 == 1 else 0.5), scalar2=None, op0=ALU.mult),
                        reads=[f"modc{l}"], writes=[f"coef{l}"])

        rr["ms"] = 0
        for s_ in range(24):
            mod_step(0, s_)
        mod_pending = [(0, s_) for s_ in range(24, 72)] + [(l_, s_) for l_ in range(1, depth) for s_ in range(72)]

        def mod_flush(l, s3):
            while mod_pending and (mod_pending[0][0] < l or (mod_pending[0][0] == l and mod_pending[0][1] < 24 * (s3 + 1))):
                l_, s_ = mod_pending.pop(0)
                mod_step(l_, s_)

        def mod_tick(cur_l):
            for _ in range(2):
                if mod_pending and mod_pending[0][0] <= cur_l + 1:
                    l_, s_ = mod_pending.pop(0)
                    mod_step(l_, s_)

        barrier()

        def norm(g, l, s):
            mod_flush(l, s)
            v = 1 if g == "P" else 0
            for tb in range(2):
                tbs = slice(tb * 512, (tb + 1) * 512)
                pst, pstk = bank("aux")
                for c in range(8):
                    sq = sqb[rr["sq"] % 2]
                    sqk = f"sq{rr['sq'] % 2}"
                    rr["sq"] += 1
                    P.op("act", lambda e, sq=sq, c=c, tbs=tbs: e.activation(out=sq[:], in_=xres[g][:, c, tbs], func=AF.Square),
                         reads=[xk(g, c, tb)], writes=[sqk])
                    P.op("pe", lambda e, sq=sq, c=c, pst=pst: e.matmul(pst[:], CB("ones1024"), sq[:], start=(c == 0), stop=(c == 7)),
                         reads=[sqk, "cstb"], writes=[pstk])
                rstd = rstdb[tb]
                rk = f"rstd{tb}"
                P.op("act", lambda e, rstd=rstd, pst=pst: e.activation(out=rstd[:], in_=pst[:], func=AF.Ln, bias=EPS),
                     reads=[pstk], writes=[rk])
                P.op("act", lambda e, rstd=rstd: e.activation(out=rstd[:], in_=rstd[:], func=AF.Exp, scale=-0.5),
                     reads=[rk], writes=[rk])
                for c in range(8):
                    tmp = tmpf[rr["tmp"] % 3]
                    tk = f"tmpf{rr['tmp'] % 3}"
                    rr["tmp"] += 1
                    P.op("dve", lambda e, tmp=tmp, c=c, rstd=rstd, tbs=tbs: e.scalar_tensor_tensor(
                        out=tmp[:], in0=xres[g][:, c, tbs], scalar=coefA[:, l, v, s, c:c + 1], in1=rstd[:],
                        op0=ALU.mult, op1=ALU.mult), reads=[xk(g, c, tb), rk, f"coef{l}"], writes=[tk])
                    shift = modc[:, l, 3 * s * 8 + c, v:v + 1]
                    if c % 2 == 0:
                        P.op("act", lambda e, tmp=tmp, c=c, shift=shift, tbs=tbs: e.activation(
                            out=hbf[:, c, tbs], in_=tmp[:], func=AF.Identity, bias=shift),
                            reads=[tk, f"modc{l}"], writes=[hk(c, tb)])
                    else:
                        P.op("dve", lambda e, tmp=tmp, c=c, shift=shift, tbs=tbs: e.tensor_scalar(
                            out=hbf[:, c, tbs], in0=tmp[:], scalar1=shift, scalar2=None, op0=ALU.add),
                            reads=[tk, f"modc{l}"], writes=[hk(c, tb)])

        hid = bigb[:, 0:22 * T].rearrange("p (j t) -> p j t", j=22)
        hall = [hk(c, tb) for c in range(8) for tb in range(2)]

        def ffn(g, l, f, s):
            v = 1 if g == "P" else 0
            Wi = ffn_w_in[l, f]
            Wo = ffn_w_out[l, f]
            for jq in range(6):
                nj = 4 if jq < 5 else 2
                j0 = jq * 4
                sa_, sak = load_slot([((lambda sl, k=k: sl[:, k:k + 4, 0:nj * 128]),
                                       Wi[k * 128:(k + 4) * 128, j0 * 128:(j0 + nj) * 128].rearrange("(c p) n -> p c n", p=128))
                                      for k in (0, 4)])
                sb_, sbk = load_slot([((lambda sl, k=k: sl[:, k:k + 4, 0:nj * 128]),
                                       Wi[k * 128:(k + 4) * 128, DFF + j0 * 128:DFF + (j0 + nj) * 128].rearrange("(c p) n -> p c n", p=128))
                                      for k in (0, 4)])
                for tb in range(2):
                    for jj in range(nj):
                        j = j0 + jj
                        tbs = slice(tb * 512, (tb + 1) * 512)
                        pa, pak = bank("mm")
                        pb_, pbk_ = bank("mm")

                        def mma(e, sl=sa_, ps=pa, jj=jj, tbs=tbs):
                            for k in range(8):
                                i_ = e.matmul(ps[:], sl[:, k, jj * 128:(jj + 1) * 128], hbf[:, k, tbs], start=(k == 0), stop=(k == 7))
                            return i_
                        P.op("pe", mma, reads=[sak] + [hk(c, tb) for c in range(8)], writes=[pak])

                        def mmb(e, sl=sb_, ps=pb_, jj=jj, tbs=tbs):
                            for k in range(8):
                                i_ = e.matmul(ps[:], sl[:, k, jj * 128:(jj + 1) * 128], hbf[:, k, tbs], start=(k == 0), stop=(k == 7))
                            return i_
                        P.op("pe", mmb, reads=[sbk] + [hk(c, tb) for c in range(8)], writes=[pbk_])
                        tmp = tmpf[rr["tmp"] % 3]
                        tk = f"tmpf{rr['tmp'] % 3}"
                        rr["tmp"] += 1
                        P.op("act", lambda e, tmp=tmp, pa=pa: e.activation(out=tmp[:], in_=pa[:], func=AF.Silu),
                             reads=[pak], writes=[tk])
                        P.op("dve", lambda e, tmp=tmp, pb_=pb_, j=j, tbs=tbs: e.tensor_tensor(
                            out=hid[:, j, tbs], in0=tmp[:], in1=pb_[:], op=ALU.mult),
                            reads=[tk, pbk_, "big"], writes=[f"hid{j}.{tb}"])
                mod_tick(l)
            for npair in range(4):
                n0 = npair * 2
                sx, sxk = load_slot([((lambda sl, j=j, g_=g_: sl[:, :, :].rearrange("p a (b c) -> p (a b) c", c=256)[:, j:j + g_, :]),
                                      Wo[j * 128:(j + g_) * 128, n0 * 128:(n0 + 2) * 128].rearrange("(c p) n -> p c n", p=128))
                                     for (j, g_) in ((0, 4), (4, 4), (8, 3))])
                sy, syk = load_slot([((lambda sl, j=j, g_=g_: sl[:, :, :].rearrange("p a (b c) -> p (a b) c", c=256)[:, j:j + g_, :]),
                                      Wo[(11 + j) * 128:(11 + j + g_) * 128, n0 * 128:(n0 + 2) * 128].rearrange("(c p) n -> p c n", p=128))
                                     for (j, g_) in ((0, 4), (4, 4), (8, 3))])
                for nn in range(2):
                    n = n0 + nn
                    for tb in range(2):
                        tbs = slice(tb * 512, (tb + 1) * 512)
                        po, pok = bank("mm")

                        def mmo(e, po=po, nn=nn, tbs=tbs, sx=sx, sy=sy):
                            fx = sx[:, :, :].rearrange("p a b -> p (a b)")
                            fy = sy[:, :, :].rearrange("p a b -> p (a b)")
                            for j in range(22):
                                fl = fx if j < 11 else fy
                                jj = j % 11
                                i_ = e.matmul(po[:], fl[:, jj * 256 + nn * 128:jj * 256 + (nn + 1) * 128], hid[:, j, tbs],
                                              start=(j == 0), stop=(j == 21))
                            return i_
                        P.op("pe", mmo, reads=[sxk, syk, "big"] + [f"hid{j}.{tb}" for j in range(22)], writes=[pok])
                        P.op("dve", lambda e, po=po, n=n, tbs=tbs: e.scalar_tensor_tensor(
                            out=xres[g][:, n, tbs], in0=po[:], scalar=coefG[:, l, v, s, n:n + 1], in1=xres[g][:, n, tbs],
                            op0=ALU.mult, op1=ALU.add), reads=[pok, f"coef{l}", xk(g, n, tb)], writes=[xk(g, n, tb)])
                mod_tick(l)

        R0 = 4096
        yT = bigb[:, 0:8 * T].rearrange("p (c t) -> p c t", c=8)
        wblk = sb("wblk", [128, 8, 128], BF16)
        lruo = sb("lruo", [128, 64])
        rvt = [sb(f"rvt{i}", [128, 128]) for i in range(2)]
        lgt = sb("lgt", [128, 8])
        gcht = sb("gcht", [128, 2, 2])
        lgp = sb("lgp", [128, 2, 2])
        ztt = sb("ztt", [128, 8])
        sfm = sb("sfm", [128, 2, 64])
        P.op("dve", lambda e: e.memset(wblk[:], 0.0), writes=["wblk"])
        P.op("dve", lambda e: e.memset(lruo[:], 0.0), writes=["lruo"])
        rr.update(dict(t=0, pt=0, rv=0, so=0))

        def W(off, n):
            return big[:, R0 + off:R0 + off + n]

        def WB(off, n):
            return bigb[:, 2 * (R0 + off):2 * (R0 + off) + n]

        TOFF = 7424
        NTMP = 6

        def tget():
            i = rr["t"] % NTMP
            rr["t"] += 1
            return W(TOFF + i * 512, 512), f"mt{i}"

        def ptget():
            i = rr["pt"] % 3
            rr["pt"] += 1
            return WB(TOFF + NTMP * 512 + i * 256, 512), f"pt{i}"

        def load_w_in(l, col0, ncols):
            return load_slot([((lambda sl, k=k: sl[:, k:k + 4, 0:ncols]),
                               w_in[l, k * 128:(k + 4) * 128, col0:col0 + ncols].rearrange("(c p) n -> p c n", p=128))
                              for k in (0, 4)])

        def proj_fm(sl, slk, co, tb, dup=False):
            pbk, pk = bank("mm")
            tbs = slice(tb * 512, (tb + 1) * 512)

            def f(e):
                for k in range(8):
                    i_ = e.matmul(pbk[:], sl[:, k, co:co + 128], hbf[:, k, tbs], start=(k == 0), stop=(k == 7))
                return i_

            def fdup(e):
                for half in range(2):
                    for k in range(8):
                        i_ = e.matmul(pbk[half * 64:(half + 1) * 64, :], sl[:, k, co:co + 64], hbf[:, k, tbs],
                                      start=(k == 0), stop=(k == 7), tile_position=(0, half * 64))
                return i_
            P.op("pe", fdup if dup else f, reads=[slk] + [hk(c, tb) for c in range(8)], writes=[pk])
            return pbk, pk

        def proj_tm(sl, slk, co, ncols, tok0):
            pbk, pk = bank("mm")

            def f(e):
                for k in range(8):
                    i_ = e.matmul(pbk[:, 0:ncols], hbf[:, k, tok0:tok0 + 128], sl[:, k, co:co + ncols],
                                  start=(k == 0), stop=(k == 7))
                return i_
            tbset = sorted({tok0 // 512, (tok0 + 127) // 512})
            P.op("pe", f, reads=[slk] + [hk(c, tb) for c in range(8) for tb in tbset], writes=[pk])
            return pbk, pk

        def rstd_from(pbk, pk):
            rs, rsk = tget()
            P.op("act", lambda e: e.activation(out=rs, in_=pbk[:], func=AF.Ln, bias=EPS), reads=[pk, "big"], writes=[rsk])
            P.op("act", lambda e: e.activation(out=rs, in_=rs, func=AF.Exp, scale=-0.5), reads=[rsk, "big"], writes=[rsk])
            return rs, rsk

        def hn_A(job):
            pbk, pk = job["proj"]()
            sqf, sqk = tget()
            P.op("act", lambda e: e.activation(out=sqf, in_=pbk[:], func=AF.Square), reads=[pk, "big"], writes=[sqk])
            job.update(pbk=pbk, pk=pk, sqf=sqf, sqk=sqk)

        def hn_B(job):
            pbk, pk, sqf, sqk = job["pbk"], job["pk"], job["sqf"], job["sqk"]
            l, gcol, dst, dstk, tb = job["l"], job["gcol"], job["dst"], job["dstk"], job["tb"]
            ss, ssk = bank("aux")
            P.op("pe", lambda e: e.matmul(ss[:], C("b64"), sqf, start=True, stop=True), reads=[sqk, "cst", "big"], writes=[ssk])
            rs, rsk = rstd_from(ss, ssk)
            if not job.get("rope") and not job.get("want_f32"):
                P.op("dve", lambda e: e.scalar_tensor_tensor(out=dst, in0=pbk[:], scalar=gcol, in1=rs, op0=ALU.mult, op1=ALU.mult),
                     reads=[pk, rsk, f"vecs{l}", "big"], writes=[dstk])
                return
            qn, qnk = tget()
            P.op("dve", lambda e: e.scalar_tensor_tensor(out=qn, in0=pbk[:], scalar=gcol, in1=rs, op0=ALU.mult, op1=ALU.mult),
                 reads=[pk, rsk, f"vecs{l}", "big"], writes=[qnk])
            if job.get("rope"):
                apply_rope((qn, qnk), tb, dst, dstk)
            else:
                P.op("act", lambda e: e.activation(out=dst, in_=qn, func=AF.Copy), reads=[qnk, "big"], writes=[dstk])
            if job.get("post"):
                job["post"]((qn, qnk))

        def run_hn(jobs):
            hn_A(jobs[0])
            for i, j in enumerate(jobs):
                if i + 1 < len(jobs):
                    hn_A(jobs[i + 1])
                hn_B(j)

        def reverse_seq(src, srck, dst, dstk, nseq, L, stg, stgk):
            nb = L // 128
            for half in range(2):
                p1, p1k = bank("aux")

                def trs(e, p1=p1, half=half):
                    for i4 in range(4):
                        b = half * 4 + i4
                        i_ = e.transpose(p1[:, i4 * 128:(i4 + 1) * 128], src[:, b * 128:(b + 1) * 128], ident)
                    return i_
                P.op("pe", trs, reads=[srck, "cst", "big"], writes=[p1k])
                copy_op(ew_engine(), stg[:, half * 512:(half + 1) * 512], p1[:], [p1k, "big"], [stgk])
            for dh in range(2):
                p2, p2k = bank("aux")

                def jm(e, p2=p2, dh=dh):
                    for i4 in range(4):
                        d = dh * 4 + i4
                        s_ = d // nb
                        sb_ = s_ * nb + (nb - 1 - (d - s_ * nb))
                        i_ = e.matmul(p2[:, i4 * 128:(i4 + 1) * 128], stg[:, sb_ * 128:(sb_ + 1) * 128], C("J"), start=True, stop=True)
                    return i_
                P.op("pe", jm, reads=[stgk, "cst", "big"], writes=[p2k])
                copy_op(ew_engine(), dst[:, dh * 512:(dh + 1) * 512], p2[:], [p2k, "big"], [dstk])

        def mixer_A(g, l):
            nseq, L = (4, 256) if g == "P" else (1, 1024)
            if g == "P":
                def ldw(e):
                    ins = []
                    for d in range(2):
                        for c in range(2):
                            for ri, Wt in enumerate((lru_w_r, lru_w_i)):
                                for half in range(2):
                                    ins.append(e.dma_start(out=wblk[half * 64:(half + 1) * 64, d * 4 + c * 2 + ri, half * 64:(half + 1) * 64],
                                                           in_=Wt[l, d, 2 * c + half]))
                    return ins
                P.op("pool", ldw, writes=["wblk"], dsem="wblk", ninc=16)
            sl, slk = load_w_in(l, 0, 512)
            bufs = [(W(i * 1024, 1024), f"lb{i}") for i in range(10)]
            (xa, xak), (ga, gak), (xc, xck), (xcr, xcrk), (B1, B1k), (B2, B2k), (B3, B3k), (B4, B4k), (B5, B5k), (B6, B6k) = bufs
            xcb, xcbk = WB(10240, 1024), "xcb"
            xcrb, xcrbk = WB(10752, 1024), "xcrb"
            for c in range(2):
                for tb in range(2):
                    tbs = slice(tb * 512, (tb + 1) * 512)
                    pbk, pk = proj_fm(sl, slk, c * 128, tb)
                    copy_op("act", xa[:, tbs], pbk[:], [pk, "big"], [xak])
                    pbk2, pk2 = proj_fm(sl, slk, 256 + c * 128, tb)
                    copy_op("dve", ga[:, tbs], pbk2[:], [pk2, "big"], [gak])
                vk = f"vecs{l}"
                P.op("dve", lambda e, c=c: e.tensor_scalar(out=xc, in0=xa, scalar1=vecs[:, l, 100 + c:101 + c], scalar2=vecs[:, l, 104 + c:105 + c],
                                                          op0=ALU.mult, op1=ALU.add), reads=[xak, vk, "big"], writes=[xck])
                xa3 = xa.rearrange("p (s t) -> p s t", s=nseq)
                xc3 = xc.rearrange("p (s t) -> p s t", s=nseq)
                for (i, so, do, n) in ((0, 0, 2, L - 2), (1, 0, 1, L - 1), (3, 1, 0, L - 1)):
                    P.op("dve", lambda e, i=i, so=so, do=do, n=n, c=c: e.scalar_tensor_tensor(
                        out=xc3[:, :, do:do + n], in0=xa3[:, :, so:so + n], scalar=vecs[:, l, 96 + i * 2 + c:96 + i * 2 + c + 1],
                        in1=xc3[:, :, do:do + n], op0=ALU.mult, op1=ALU.add), reads=[xak, xck, vk, "big"], writes=[xck])
                copy_op("act", xcb, xc, [xck, "big"], [xcbk])
                reverse_seq(xc, xck, xcr, xcrk, nseq, L, B1, B1k)
                copy_op("act", xcrb, xcr, [xcrk, "big"], [xcrbk])
                for d in range(2):
                    src, srck, srcb, srcbk = (xc, xck, xcb, xcbk) if d == 0 else (xcr, xcrk, xcrb, xcrbk)
                    A_, A_k = (B1, B1k) if d == 0 else (B5, B5k)
                    U_, U_k = (B2, B2k) if d == 0 else (B6, B6k)
                    H_, H_k = (B3, B3k) if d == 0 else (B4, B4k)
                    dc = d * 2 + c
                    for tb in range(2):
                        tbs = slice(tb * 512, (tb + 1) * 512)
                        for ri, (dst, dstk, bcol) in enumerate(((A_, A_k, 106 + dc), (U_, U_k, 110 + dc))):
                            pbk, pk = bank("mm")
                            P.op("pe", lambda e, pbk=pbk, ri=ri, tbs=tbs, srcb=srcb, d=d, c=c: e.matmul(
                                pbk[:], wblk[:, d * 4 + c * 2 + ri, :], srcb[:, tbs], start=True, stop=True),
                                reads=["wblk", srcbk, "big"], writes=[pk])
                            P.op("act", lambda e, pbk=pbk, dst=dst, tbs=tbs, bcol=bcol: e.activation(
                                out=dst[:, tbs], in_=pbk[:], func=AF.Sigmoid, bias=vecs[:, l, bcol:bcol + 1]),
                                reads=[pk, vk, "big"], writes=[dstk])
                    P.op("act", lambda e, H_=H_, A_=A_, dc=dc: e.activation(out=H_, in_=A_, func=AF.Exp, scale=lruc[:, l, 1, dc:dc + 1]),
                         reads=[A_k, "lruc", "big"], writes=[H_k])
                    P.op("act", lambda e, A_=A_, dc=dc: e.activation(out=A_, in_=A_, func=AF.Exp, scale=lruc[:, l, 0, dc:dc + 1]),
                         reads=[A_k, "lruc", "big"], writes=[A_k])
                    P.op("dve", lambda e, H_=H_: e.tensor_scalar(out=H_, in0=H_, scalar1=-1.0, scalar2=1.0, op0=ALU.mult, op1=ALU.add),
                         reads=[H_k, "big"], writes=[H_k])
                    P.op("act", lambda e, H_=H_: e.activation(out=H_, in_=H_, func=AF.Sqrt), reads=[H_k, "big"], writes=[H_k])
                    P.op("dve", lambda e, U_=U_, src=src: e.tensor_tensor(out=U_, in0=U_, in1=src, op=ALU.mult),
                         reads=[U_k, srck, "big"], writes=[U_k])
                    P.op("dve", lambda e, U_=U_, H_=H_: e.tensor_tensor(out=U_, in0=U_, in1=H_, op=ALU.mult),
                         reads=[U_k, H_k, "big"], writes=[U_k])
                    for s in range(nseq):
                        ss_ = slice(s * L, (s + 1) * L)
                        init = 0.0 if g == "P" else vecs[:, l, 124 + dc:125 + dc]
                        P.op("dve", lambda e, H_=H_, A_=A_, U_=U_, ss_=ss_, init=init: e.tensor_tensor_scan(
                            out=H_[:, ss_], data0=A_[:, ss_], data1=U_[:, ss_], initial=init, op0=ALU.mult, op1=ALU.add),
                            reads=[A_k, U_k, vk, "big"], writes=[H_k])
                        if g == "P":
                            col = ((s * DEPTH + l) * 2 + d) * 2 + c
                            P.op("dve", lambda e, H_=H_, s=s, col=col: e.tensor_copy(out=lruo[:, col:col + 1], in_=H_[:, (s + 1) * L - 1:(s + 1) * L]),
                                 reads=[H_k, "big"], writes=["lruo"])
                reverse_seq(B4, B4k, B1, B1k, nseq, L, B5, B5k)
                P.op("dve", lambda e: e.tensor_tensor(out=B3, in0=B3, in1=B1, op=ALU.add), reads=[B3k, B1k, "big"], writes=[B3k])
                P.op("dve", lambda e: e.tensor_tensor(out=B2, in0=ga, in1=ga, op=ALU.mult), reads=[gak, "big"], writes=[B2k])
                P.op("dve", lambda e: e.tensor_scalar(out=B2, in0=B2, scalar1=0.044715, scalar2=1.0, op0=ALU.mult, op1=ALU.add),
                     reads=[B2k, "big"], writes=[B2k])
                P.op("dve", lambda e: e.tensor_tensor(out=B2, in0=B2, in1=ga, op=ALU.mult), reads=[B2k, gak, "big"], writes=[B2k])
                P.op("act", lambda e: e.activation(out=B2, in_=B2, func=AF.Sigmoid, scale=1.5957691216057308),
                     reads=[B2k, "big"], writes=[B2k])
                P.op("dve", lambda e: e.tensor_tensor(out=B3, in0=B3, in1=ga, op=ALU.mult), reads=[B3k, gak, "big"], writes=[B3k])
                P.op("dve", lambda e, c=c: e.tensor_tensor(out=yT[:, c, :], in0=B3, in1=B2, op=ALU.mult),
                     reads=[B3k, B2k, "big"], writes=[f"y{c}"])

        def rope_tabs(tb):
            tc_, tck = tget()
            ts_, tsk = tget()
            r0 = tb * 8
            P.op("dve", lambda e: e.tensor_tensor(out=tc_.rearrange("p (r c) -> p r c", r=8),
                                                  in0=C("CA")[:, r0:r0 + 8].unsqueeze(2).to_broadcast([128, 8, 64]),
                                                  in1=C("CB").unsqueeze(1).to_broadcast([128, 8, 64]), op=ALU.mult),
                 reads=["cst", "big"], writes=[tck])
            P.op("dve", lambda e: e.tensor_tensor(out=ts_.rearrange("p (r c) -> p r c", r=8),
                                                  in0=C("SA")[:, r0:r0 + 8].unsqueeze(2).to_broadcast([128, 8, 64]),
                                                  in1=C("SB").unsqueeze(1).to_broadcast([128, 8, 64]), op=ALU.add),
                 reads=["cst", "big"], writes=[tsk])
            return (tc_, tck), (ts_, tsk)

        def softmax_attend(qT, kfn, vfn, q_ranges, keylist, ych, extra_reads):
            import os
            if os.environ.get("KDBG_NOATT"):
                return
            for qc in range(2):
                for (q0, qn_) in q_ranges:
                    po, pok = bank("mm")
                    pd, pdk = bank("mm")
                    keys = keylist(q0)
                    items = []
                    for hp in range(2):
                        h = 2 * qc + hp
                        for ki, kd in enumerate(keys):
                            items.append((hp, h, ki, kd))
                    state = {}

                    def emit_sc(i, po=po, pd=pd, q0=q0, qn_=qn_, qc=qc, keys=keys):
                        hp, h, ki, kd = items[i]
                        ps, psk = bank("aux")
                        lhsT = kfn(h, hp, kd)
                        P.op("pe", lambda e, ps=ps, lhsT=lhsT, hp=hp: e.matmul(
                            ps[:, 0:qn_], lhsT, qT[hp * 64:(hp + 1) * 64, qc, q0:q0 + qn_], start=True, stop=True),
                            reads=["mq", "mk", "big"] + extra_reads, writes=[psk])
                        pT, pTk = ptget()
                        P.op("act", lambda e, ps=ps, pT=pT: e.activation(out=pT[:, 0:qn_], in_=ps[:, 0:qn_], func=AF.Exp, scale=0.125),
                             reads=[psk, "big"], writes=[pTk])
                        state[i] = (pT, pTk)

                    def emit_pv(i, po=po, pd=pd, pok=pok, pdk=pdk, qn_=qn_, keys=keys):
                        hp, h, ki, kd = items[i]
                        pT, pTk = state.pop(i)
                        vv = vfn(h, kd)
                        first = (ki == 0)
                        last = (ki == len(keys) - 1)

                        def pv(e):
                            e.matmul(po[hp * 64:(hp + 1) * 64, 0:qn_], vv, pT[:, 0:qn_], start=first, stop=last,
                                     tile_position=(0, hp * 64))
                            return e.matmul(pd[hp * 64:(hp + 1) * 64, 0:qn_], CB("ones"), pT[:, 0:qn_], start=first, stop=last,
                                            tile_position=(0, hp * 64))
                        P.op("pe", pv, reads=[pTk, "mv", "cstb", "big"] + extra_reads, writes=[pok, pdk])
                    emit_sc(0)
                    if len(items) > 1:
                        emit_sc(1)
                    for i in range(len(items)):
                        if i + 2 < len(items):
                            emit_sc(i + 2)
                        emit_pv(i)
                    rec, reck = tget()
                    P.op("dve", lambda e, rec=rec, pd=pd, qn_=qn_: e.reciprocal(out=rec[:, 0:qn_], in_=pd[:, 0:qn_]),
                         reads=[pdk, "big"], writes=[reck])
                    P.op("dve", lambda e, rec=rec, po=po, qc=qc, q0=q0, qn_=qn_: e.tensor_tensor(
                        out=yT[:, ych + qc, q0:q0 + qn_], in0=po[:, 0:qn_], in1=rec[:, 0:qn_], op=ALU.mult),
                        reads=[pok, reck, "big"], writes=[f"y{ych + qc}"])

        def mixer_attn(g, l, kind):
            isB = kind == "B"
            isP = g == "P"
            qcol0 = 512 if isB else 1024
            kw = 128 if isB else 256
            vw = kw
            ych = 2 if isB else 4
            gq = 120 if isB else 122
            gk = gq + 1
            kout, vout = (nbk, nbv) if isB else (nck, ncv)
            sl, slk = load_w_in(l, qcol0, 256 + kw)
            slv, slvk = load_w_in(l, qcol0 + 256 + kw, vw)
            qT = WB(0, 2048).rearrange("p (c t) -> p c t", c=2)
            kT = WB(1024, 2048).rearrange("p (c t) -> p c t", c=2)
            vtok = WB(2048, 8 * vw).rearrange("p (b c) -> p b c", b=8)
            kst = [W(3200 + i * 1024, 4 * kw).rearrange("p (b c) -> p b c", b=4) for i in range(2)]
            vst = [W(5248 + i * 1024, 4 * vw).rearrange("p (b c) -> p b c", b=4) for i in range(2)]
            rope = not isP
            vk = f"vecs{l}"
            jobs = []
            for tb in range(2):
                tbs = slice(tb * 512, (tb + 1) * 512)
                for qc in range(2):
                    jobs.append(dict(proj=(lambda qc=qc, tb=tb: proj_fm(sl, slk, qc * 128, tb)), l=l, gcol=vecs[:, l, gq:gq + 1], tb=tb,
                                     dst=qT[:, qc, tbs], dstk="mq", rope=rope, want_f32=rope))
                for kc in range(2):
                    def post(r, kc=kc, tb=tb):
                        qn, qnk = r
                        pt_, ptk = bank("aux")
                        wd = 64 if isB else 128

                        def trk(e):
                            for b in range(4):
                                i_ = e.transpose(pt_[:, b * wd:(b + 1) * wd], qn[0:wd, b * 128:(b + 1) * 128], ident[0:wd, 0:wd])
                            return i_
                        P.op("pe", trk, reads=[qnk, "cst", "big"], writes=[ptk])
                        copy_op("dve", kst[tb][:, :, kc * wd:(kc + 1) * wd], pt_[:, 0:4 * wd].rearrange("p (b c) -> p b c", b=4),
                                [ptk, "big"], [f"kst{tb}"])
                        if kc == 1:
                            P.op("sp", lambda e: [e.dma_start(
                                out=kout[2 * tb + s_, l].rearrange("(b p) c -> p b c", p=128), in_=kst[tb][:, 2 * s_:2 * s_ + 2, :]) for s_ in range(2)],
                                reads=[f"kst{tb}", "big"], dsem=f"kst{kind}{tb}", ninc=2)
                    jobs.append(dict(proj=(lambda kc=kc, tb=tb: proj_fm(sl, slk, 256 + (kc * 64 if isB else kc * 128), tb, dup=isB)), l=l,
                                     gcol=vecs[:, l, gk:gk + 1], tb=tb, dst=kT[:, kc, tbs], dstk="mk", rope=rope, want_f32=True,
                                     post=(post if isP else None)))
            run_hn(jobs)
            for blk in range(8):
                tb = blk // 4
                pbk, pk = proj_tm(slv, slvk, 0, vw, blk * 128)
                copy_op("act", vtok[:, blk, :], pbk[:, 0:vw], [pk, "big"], ["mv"])
                if isP:
                    copy_op("dve", vst[tb][:, blk % 4, :], pbk[:, 0:vw], [pk, "big"], [f"vst{tb}"])
                    if blk % 4 == 3:
                        P.op("sp", lambda e, tb=tb: [e.dma_start(
                            out=vout[2 * tb + s_, l].rearrange("(b p) c -> p b c", p=128), in_=vst[tb][:, 2 * s_:2 * s_ + 2, :]) for s_ in range(2)],
                            reads=[f"vst{tb}", "big"], dsem=f"vst{kind}{tb}", ninc=2)
            if isP:
                keylist = lambda q0: [("l", q0 // 128), ("l", q0 // 128 + 1)]
                q_ranges = [(s * 256, 256) for s in range(4)]
                kfn = lambda h, hp, kd: kT[hp * 64:(hp + 1) * 64, h // 2, kd[1] * 128:(kd[1] + 1) * 128]
                vfn = lambda h, kd: vtok[:, kd[1], ((h // 2) if isB else h) * 64:((h // 2) if isB else h) * 64 + 64]
                softmax_attend(qT, kfn, vfn, q_ranges, keylist, ych, [])
            else:
                kctx = WB(3200, 512).rearrange("p (c t) -> p c t", c=2)
                vctx = WB(3456, 256).rearrange("p (b c) -> p b c", b=2)
                cstg = W(3584, 256).rearrange("p (b c) -> p b c", b=2)
                P.op("sp", lambda e: e.dma_start(out=cstg, in_=cbk[l].rearrange("(b p) c -> p b c", p=128)),
                     reads=["big"], writes=["cstg"], dsem="cstg")
                P.op("pool", lambda e: [e.dma_start(out=vctx[:, b, :], in_=cbv[l, b * 128:(b + 1) * 128, :]) for b in range(2)],
                     reads=["big"], writes=["vctx"], dsem="vctx", ninc=2)
                for kv in range(2):
                    pt_, ptk = bank("aux")

                    def trc(e, pt_=pt_, kv=kv):
                        for b in range(2):
                            for half in range(2):
                                i_ = e.matmul(pt_[half * 64:(half + 1) * 64, b * 128:(b + 1) * 128],
                                              cstg[:, b, kv * 64:(kv + 1) * 64], ident, start=True, stop=True,
                                              tile_position=(0, half * 64))
                        return i_
                    P.op("pe", trc, reads=["cstg", "cst", "big"], writes=[ptk])
                    copy_op("act", kctx[:, kv, :], pt_[:, 0:256], [ptk, "big"], ["kctx"])
                keylist = lambda q0: [("l", b) for b in range(8)] + [("c", b) for b in range(2)]
                q_ranges = [(0, 512), (512, 512)]

                def kfn(h, hp, kd):
                    if kd[0] == "l":
                        return kT[hp * 64:(hp + 1) * 64, h // 2, kd[1] * 128:(kd[1] + 1) * 128]
                    return kctx[hp * 64:(hp + 1) * 64, h // 2, kd[1] * 128:(kd[1] + 1) * 128]

                def vfn(h, kd):
                    kv = h // 2
                    if kd[0] == "l":
                        return vtok[:, kd[1], kv * 64:(kv + 1) * 64]
                    return vctx[:, kd[1], kv * 64:(kv + 1) * 64]
                softmax_attend(qT, kfn, vfn, q_ranges, keylist, ych, ["kctx", "vctx"])

        def apply_rope(r, tb, dst, dstk):
            qn, qnk = r
            (tc_, tck), (ts_, tsk) = rope_tabs(tb)
            sw, swk = bank("aux")
            P.op("pe", lambda e: e.matmul(sw[:], C("pswap"), qn, start=True, stop=True), reads=[qnk, "cst", "big"], writes=[swk])
            P.op("dve", lambda e: e.tensor_tensor(out=ts_, in0=sw[:], in1=ts_, op=ALU.mult), reads=[swk, tsk, "big"], writes=[tsk])
            P.op("dve", lambda e: e.tensor_tensor(out=tc_, in0=qn, in1=tc_, op=ALU.mult), reads=[qnk, tck, "big"], writes=[tck])
            P.op("dve", lambda e: e.tensor_tensor(out=dst, in0=tc_, in1=ts_, op=ALU.add), reads=[tck, tsk, "big"], writes=[dstk])

        def mixer_nat(l):
            sl, slk = load_w_in(l, 1024, 512)
            slv, slvk = load_w_in(l, 1536, 256)
            qT = WB(0, 2048).rearrange("p (c t) -> p c t", c=2)
            kT = WB(1024, 2048).rearrange("p (c t) -> p c t", c=2)
            vtok = WB(2048, 2048).rearrange("p (b c) -> p b c", b=8)
            vsh = WB(3072, 1792).rearrange("p (b c) -> p b c", b=7)
            kctx = WB(3968, 512).rearrange("p (c t) -> p c t", c=2)
            vctx = WB(4224, 512).rearrange("p (b c) -> p b c", b=2)
            cstg = W(4480, 512).rearrange("p (b c) -> p b c", b=2)
            bias2 = WB(4992, 3584).rearrange("p (h i c) -> p h i c", h=4, i=14)
            tabr = W(6784, 32)
            tabT = W(6848, 64)
            vk = f"vecs{l}"
            jobs = []
            for tb in range(2):
                tbs = slice(tb * 512, (tb + 1) * 512)
                for qc in range(2):
                    jobs.append(dict(proj=(lambda qc=qc, tb=tb: proj_fm(sl, slk, qc * 128, tb)), l=l, gcol=vecs[:, l, 122:123], tb=tb,
                                     dst=qT[:, qc, tbs], dstk="mq"))
                for kc in range(2):
                    jobs.append(dict(proj=(lambda kc=kc, tb=tb: proj_fm(sl, slk, 256 + kc * 128, tb)), l=l, gcol=vecs[:, l, 123:124], tb=tb,
                                     dst=kT[:, kc, tbs], dstk="mk"))
            run_hn(jobs)
            for blk in range(8):
                pbk, pk = proj_tm(slv, slvk, 0, 256, blk * 128)
                copy_op(ew_engine(), vtok[:, blk, :], pbk[:, 0:256], [pk, "big"], ["mv"])
            for i in range(7):
                pbk, pk = proj_tm(slv, slvk, 0, 256, 64 + i * 128)
                copy_op(ew_engine(), vsh[:, i, :], pbk[:, 0:256], [pk, "big"], ["mv"])
            P.op("sp", lambda e: e.dma_start(out=cstg, in_=cck[l].rearrange("(b p) c -> p b c", p=128)),
                 reads=["big"], writes=["cstg"], dsem="cstg")
            P.op("pool", lambda e: [e.dma_start(out=vctx[:, b, :], in_=ccv[l, b * 128:(b + 1) * 128, :]) for b in range(2)],
                 reads=["big"], writes=["vctx"], dsem="vctx", ninc=2)
            for ch in range(2):
                pt_, ptk = bank("aux")

                def trc(e, pt_=pt_, ch=ch):
                    for b in range(2):
                        i_ = e.transpose(pt_[:, b * 128:(b + 1) * 128], cstg[:, b, ch * 128:(ch + 1) * 128], ident)
                    return i_
                P.op("pe", trc, reads=["cstg", "cst", "big"], writes=[ptk])
                copy_op("act", kctx[:, ch, :], pt_[:, 0:256], [ptk, "big"], ["kctx"])
            P.op("sp", lambda e: e.dma_start(out=tabr[0:60, 0:31], in_=nat_bias[l].rearrange("h d m -> (h d) m")),
                 reads=["big"], writes=["tabr"], dsem="tabr")
            pt_, ptk = bank("aux")
            P.op("pe", lambda e, pt_=pt_: e.transpose(pt_[0:31, 0:60], tabr[0:60, 0:31], ident[0:60, 0:60]),
                 reads=["tabr", "cst", "big"], writes=[ptk])
            copy_op("dve", tabT[0:31, 0:60], pt_[0:31, 0:60], [ptk, "big"], ["tabT"])
            tab3 = tabT[0:31, 0:60].rearrange("p (h d) -> p h d", h=4)
            tabA = W(6912, 64)
            tabB = W(6976, 64)
            P.op("dve", lambda e: e.tensor_copy(out=tabA[0:31, 0:56].rearrange("p (h i) -> p h i", h=4), in_=tab3[:, :, 0:14]),
                 reads=["tabT", "big"], writes=["tabA"])
            P.op("dve", lambda e: e.tensor_copy(out=tabB[0:31, 0:56].rearrange("p (h i) -> p h i", h=4), in_=tab3[:, :, 1:15]),
                 reads=["tabT", "big"], writes=["tabA"])
            hb_ = C("hA")[0:31, 0:128]
            for cg in range(8):
                pbk, pk = bank("aux")

                def mkb(e, pbk=pbk, cg=cg):
                    for ci in range(8):
                        c = cg * 8 + ci
                        e.matmul(pbk[0:64, ci * 56:(ci + 1) * 56], hb_[:, 63 - c:127 - c], tabA[0:31, 0:56], start=True, stop=True,
                                 tile_position=(0, 0))
                        i_ = e.matmul(pbk[64:128, ci * 56:(ci + 1) * 56], hb_[:, 63 - c:127 - c], tabB[0:31, 0:56], start=True, stop=True,
                                      tile_position=(0, 64))
                    return i_
                P.op("pe", mkb, reads=["tabA", "cst", "big"], writes=[pk])
                P.op("dve", lambda e, pbk=pbk, cg=cg: e.tensor_tensor(
                    out=bias2[:, :, :, cg * 8:(cg + 1) * 8].rearrange("p h i c -> p c h i"),
                    in0=pbk[:, 0:448].rearrange("p (c h i) -> p c h i", c=8, h=4),
                    in1=C("maskC")[:, cg * 8:(cg + 1) * 8].unsqueeze(2).unsqueeze(3).to_broadcast([128, 8, 4, 14]), op=ALU.add),
                    reads=[pk, "cst", "big"], writes=["bias2"])
            for qc in range(2):
                for tb in range(2):
                    tbs = slice(tb * 512, (tb + 1) * 512)
                    po, pok = bank("mm")
                    pd, pdk = bank("mm")
                    items = [(rr_, hp) for rr_ in range(8) for hp in range(2)]
                    state = {}

                    def emit_sc(i, tb=tb, qc=qc):
                        rr_, hp = items[i]
                        r = tb * 8 + rr_
                        rs_ = min(max(r - 4, 0), 8)
                        dr0 = rs_ - r + 7
                        h = 2 * qc + hp
                        ps, psk = bank("aux")
                        qv = qT[hp * 64:(hp + 1) * 64, qc, r * 64:(r + 1) * 64]

                        def sc(e):
                            for jj in range(4):
                                k0 = (rs_ + 2 * jj) * 64
                                e.matmul(ps[:, jj * 64:(jj + 1) * 64], kT[hp * 64:(hp + 1) * 64, qc, k0:k0 + 128], qv, start=True, stop=True)
                            for b in range(2):
                                i_ = e.matmul(ps[:, 256 + b * 64:256 + (b + 1) * 64], kctx[hp * 64:(hp + 1) * 64, qc, b * 128:(b + 1) * 128], qv,
                                              start=True, stop=True)
                            return i_
                        P.op("pe", sc, reads=["mq", "mk", "kctx", "big"], writes=[psk])
                        tmp, tmpk = tget()
                        P.op("dve", lambda e: e.scalar_tensor_tensor(
                            out=tmp[:, 0:256].rearrange("p (j c) -> p j c", j=4), in0=ps[:, 0:256].rearrange("p (j c) -> p j c", j=4),
                            scalar=0.125, in1=bias2[:, h, dr0:dr0 + 7:2, :], op0=ALU.mult, op1=ALU.add),
                            reads=[psk, "bias2", "big"], writes=[tmpk])
                        pT, pTk = ptget()
                        P.op("act", lambda e: e.activation(out=pT[:, 256:384], in_=ps[:, 256:384], func=AF.Exp, scale=0.125),
                             reads=[psk, "big"], writes=[pTk])
                        P.op("act", lambda e: e.activation(out=pT[:, 0:256], in_=tmp[:, 0:256], func=AF.Exp),
                             reads=[tmpk, "big"], writes=[pTk])
                        state[i] = (pT, pTk, rs_, h)

                    def emit_pv(i, po=po, pd=pd, pok=pok, pdk=pdk):
                        rr_, hp = items[i]
                        pT, pTk, rs_, h = state.pop(i)

                        def pv(e):
                            oc = slice(rr_ * 64, (rr_ + 1) * 64)
                            for j in range(6):
                                if j < 4:
                                    krow = rs_ + 2 * j
                                    vv = vtok[:, krow // 2, h * 64:(h + 1) * 64] if krow % 2 == 0 else vsh[:, (krow - 1) // 2, h * 64:(h + 1) * 64]
                                else:
                                    vv = vctx[:, j - 4, h * 64:(h + 1) * 64]
                                e.matmul(po[hp * 64:(hp + 1) * 64, oc], vv, pT[:, j * 64:(j + 1) * 64], start=(j == 0), stop=(j == 5),
                                         tile_position=(0, hp * 64))
                                i_ = e.matmul(pd[hp * 64:(hp + 1) * 64, oc], CB("ones"), pT[:, j * 64:(j + 1) * 64], start=(j == 0), stop=(j == 5),
                                              tile_position=(0, hp * 64))
                            return i_
                        P.op("pe", pv, reads=[pTk, "mv", "vctx", "cstb", "big"], writes=[pok, pdk])
                    emit_sc(0)
                    if len(items) > 1:
                        emit_sc(1)
                    for i in range(len(items)):
                        if i + 2 < len(items):
                            emit_sc(i + 2)
                        emit_pv(i)
                    rec, reck = tget()
                    P.op("dve", lambda e, rec=rec, pd=pd: e.reciprocal(out=rec, in_=pd[:]), reads=[pdk, "big"], writes=[reck])
                    P.op("dve", lambda e, rec=rec, po=po, qc=qc, tbs=tbs: e.tensor_tensor(out=yT[:, 4 + qc, tbs], in0=po[:], in1=rec, op=ALU.mult),
                         reads=[pok, reck, "big"], writes=[f"y{4 + qc}"])

        def mixer_ret(g, l):
            isP = g == "P"
            n = 2 if isP else 8
            nseq = 4 if isP else 1
            vk = f"vecs{l}"
            if isP:
                P.op("sp", lambda e: e.dma_start(out=lgt[:], in_=bass.AP(ret_decay.tensor, l * 8, [[0, 128], [1, 8]])),
                     writes=["lgt"], dsem="lgt")
                P.op("act", lambda e: e.activation(out=lgt[:], in_=lgt[:], func=AF.Exp, scale=-1.0), reads=["lgt"], writes=["lgt"])
                P.op("act", lambda e: e.activation(out=lgt[:], in_=lgt[:], func=AF.Ln, bias=1.0), reads=["lgt"], writes=["lgt"])
                P.op("dve", lambda e: e.tensor_scalar(out=lgt[:], in0=lgt[:], scalar1=-1.0, scalar2=None, op0=ALU.mult), reads=["lgt"], writes=["lgt"])
                lg4 = lgt[:].rearrange("p (d q hp) -> p d q hp", d=2, q=2)
                for hp in range(2):
                    P.op("dve", lambda e, hp=hp: e.tensor_copy(out=lgp[hp * 64:(hp + 1) * 64, :, :], in_=lg4[hp * 64:(hp + 1) * 64, :, :, hp]),
                         reads=["lgt"], writes=["lgp"])
                P.op("act", lambda e: e.activation(out=gcht[:], in_=lgp[:], func=AF.Exp, scale=128.0), reads=["lgp"], writes=["gcht"])
                for d in range(2):
                    for h in range(4):
                        P.op("act", lambda e, d=d, h=h: e.activation(out=ztt[:, d * 4 + h:d * 4 + h + 1], in_=C("zf" if d == 0 else "zb"),
                                                                     func=AF.Exp, scale=lgt[:, d * 4 + h:d * 4 + h + 1]),
                             reads=["lgt", "cst"], writes=["ztt"])
                P.op("dve", lambda e: e.tensor_scalar(out=ztt[:], in0=ztt[:], scalar1=0.125, scalar2=None, op0=ALU.mult), reads=["ztt"], writes=["ztt"])
            s1, s1k = load_w_in(l, 1792, 512)
            s2, s2k = load_w_in(l, 2304, 256)
            qT = WB(0, 2048).rearrange("p (c t) -> p c t", c=2)
            kT = WB(1024, 2048).rearrange("p (c t) -> p c t", c=2)
            vtok = WB(2048, 2048).rearrange("p (b c) -> p b c", b=8)
            dmc = W(3072, 512).rearrange("p (h q) -> p h q", h=4)
            xit = W(3584, 256).rearrange("p (d q) -> p d q", d=2)
            qx = WB(3840, 2048).rearrange("p (d t) -> p d t", d=2)
            kz = WB(4864, 2048).rearrange("p (d b c) -> p d b c", d=2, b=8)
            Sb = WB(6144, 2048).rearrange("p (d b h e) -> p d b h e", d=2, b=8, h=2)
            for c in range(2):
                for tb in range(2):
                    tbs = slice(tb * 512, (tb + 1) * 512)
                    pbk, pk = proj_fm(s1, s1k, c * 128, tb)
                    copy_op("act", qT[:, c, tbs], pbk[:], [pk, "big"], ["mq"])
                    pbk, pk = proj_fm(s1, s1k, 256 + c * 128, tb)
                    copy_op("dve", kT[:, c, tbs], pbk[:], [pk, "big"], ["mk"])
            for blk in range(8):
                pbk, pk = proj_tm(s2, s2k, 0, 256, blk * 128)
                copy_op(ew_engine(), vtok[:, blk, :], pbk[:, 0:256], [pk, "big"], ["mv"])
            for h in range(4):
                t0, t0k = tget()
                P.op("act", lambda e, t0=t0, h=h: e.activation(out=t0[:, 0:128], in_=C("diffF"), func=AF.Exp, scale=lgt[:, h:h + 1]),
                     reads=["lgt", "cst", "big"], writes=[t0k])
                P.op("act", lambda e, t0=t0, h=h: e.activation(out=t0[:, 128:256], in_=C("diffB"), func=AF.Exp, scale=lgt[:, 4 + h:5 + h]),
                     reads=["lgt", "cst", "big"], writes=[t0k])
                P.op("dve", lambda e, t0=t0: e.tensor_tensor(out=t0[:, 0:128], in0=t0[:, 0:128], in1=C("maskF"), op=ALU.mult),
                     reads=[t0k, "cst", "big"], writes=[t0k])
                P.op("dve", lambda e, t0=t0: e.tensor_tensor(out=t0[:, 128:256], in0=t0[:, 128:256], in1=C("maskB"), op=ALU.mult),
                     reads=[t0k, "cst", "big"], writes=[t0k])
                P.op("dve", lambda e, t0=t0, h=h: e.tensor_tensor(out=dmc[:, h, :], in0=t0[:, 0:128], in1=t0[:, 128:256], op=ALU.add),
                     reads=[t0k, "big"], writes=["dmc"])
            for qc in range(2):
                for d in range(2):
                    P.op("act", lambda e, d=d, qc=qc: e.activation(out=xit[:, d, :], in_=C("xif" if d == 0 else "xib"), func=AF.Exp,
                                                                   scale=lgp[:, d, qc:qc + 1]), reads=["lgp", "cst", "big"], writes=["xit"])
                    P.op("dve", lambda e, d=d, qc=qc: e.tensor_tensor(
                        out=qx[:, d, :].rearrange("p (b q) -> p b q", b=8), in0=qT[:, qc, :].rearrange("p (b q) -> p b q", b=8),
                        in1=xit[:, d, :].unsqueeze(1).to_broadcast([128, 8, 128]), op=ALU.mult), reads=["xit", "mq", "big"], writes=["qx"])
                for blk in range(8):
                    pbk, pk = proj_tm(s1, s1k, 256 + qc * 128, 128, blk * 128)
                    for d in range(2):
                        P.op("dve", lambda e, pbk=pbk, d=d, blk=blk, qc=qc: e.tensor_tensor(
                            out=kz[:, d, blk, :].rearrange("p (h f) -> p h f", h=2), in0=pbk[:, 0:128].rearrange("p (h f) -> p h f", h=2),
                            in1=ztt[:, d * 4 + 2 * qc:d * 4 + 2 * qc + 2].unsqueeze(2).to_broadcast([128, 2, 64]), op=ALU.mult),
                            reads=[pk, "ztt", "big"], writes=["kz"])
                P.op("dve", lambda e: e.memset(Sb, 0.0), reads=["big"], writes=["Sb"])
                for s in range(nseq):
                    for d in range(2):
                        if isP:
                            P.op("dve", lambda e, d=d: e.memset(sfm[:, d, :], 0.0), writes=[f"sfm{d}"])
                        else:
                            P.op("sp", lambda e, d=d, qc=qc: e.dma_start(
                                out=sfm[:, d, :], in_=sret[l, d, 2 * qc:2 * qc + 2].rearrange("h p e -> (h p) e")),
                                writes=[f"sfm{d}"], dsem=f"sfm{d}")
                    for step in range(n):
                        for d in range(2):
                            i = step if d == 0 else n - 1 - step
                            blk = s * n + i
                            for hp in range(2):
                                copy_op("act", Sb[hp * 64:(hp + 1) * 64, d, blk, hp, :], sfm[hp * 64:(hp + 1) * 64, d, :],
                                        [f"sfm{d}", "big"], ["Sb"])
                            ps, psk = bank("aux")

                            def su(e, ps=ps, d=d, blk=blk, qc=qc):
                                for hp in range(2):
                                    i_ = e.matmul(ps[hp * 64:(hp + 1) * 64, 0:64], kz[:, d, blk, hp * 64:(hp + 1) * 64],
                                                  vtok[:, blk, (2 * qc + hp) * 64:(2 * qc + hp + 1) * 64], start=True, stop=True,
                                                  tile_position=(0, hp * 64))
                                return i_
                            P.op("pe", su, reads=["kz", "mv", "big"], writes=[psk])
                            P.op("dve", lambda e, ps=ps, d=d, qc=qc: e.scalar_tensor_tensor(
                                out=sfm[:, d, :], in0=sfm[:, d, :], scalar=gcht[:, d, qc:qc + 1], in1=ps[:, 0:64], op0=ALU.mult, op1=ALU.add),
                                reads=[psk, f"sfm{d}", "gcht"], writes=[f"sfm{d}"])
                    if isP:
                        for d in range(2):
                            P.op("sp", lambda e, d=d, s=s, qc=qc: e.dma_start(
                                out=nret[s, l, d, 2 * qc:2 * qc + 2].rearrange("h p e -> (h p) e"), in_=sfm[:, d, :]),
                                reads=[f"sfm{d}"], dsem=f"nret{d}")
                sg_, sgk = load_w_in(l, 2560 + qc * 128, 128)
                pos = []
                for tb in range(2):
                    tbs = slice(tb * 512, (tb + 1) * 512)
                    po, pok = bank("mm")
                    pos.append((po, pok))
                    items = [(b4, hp) for b4 in range(4) for hp in range(2)]
                    state = {}

                    def emit_sc(i, tb=tb, qc=qc):
                        b4, hp = items[i]
                        blk = tb * 4 + b4
                        bs = slice(blk * 128, (blk + 1) * 128)
                        h = 2 * qc + hp
                        ps, psk = bank("aux")
                        P.op("pe", lambda e: e.matmul(
                            ps[:, 0:128], kT[hp * 64:(hp + 1) * 64, qc, bs], qT[hp * 64:(hp + 1) * 64, qc, bs], start=True, stop=True),
                            reads=["mq", "mk", "big"], writes=[psk])
                        smi = rr["pt"] % 4
                        rr["pt"] += 1
                        sm, smk = WB(5888 + smi * 64, 128), f"sm{smi}"
                        P.op("dve", lambda e: e.tensor_tensor(out=sm[:, 0:128], in0=ps[:, 0:128], in1=dmc[:, h, :], op=ALU.mult),
                             reads=[psk, "dmc", "big"], writes=[smk])
                        state[i] = (sm, smk)

                    def emit_om(i, po=po, pok=pok, tb=tb, qc=qc):
                        b4, hp = items[i]
                        blk = tb * 4 + b4
                        bs = slice(blk * 128, (blk + 1) * 128)
                        h = 2 * qc + hp
                        sm, smk = state.pop(i)

                        def om(e):
                            oc = slice(b4 * 128, (b4 + 1) * 128)
                            e.matmul(po[hp * 64:(hp + 1) * 64, oc], vtok[:, blk, h * 64:(h + 1) * 64], sm[:, 0:128], start=True, stop=False,
                                     tile_position=(0, hp * 64))
                            e.matmul(po[hp * 64:(hp + 1) * 64, oc], Sb[:, 0, blk, hp, :], qx[:, 0, bs],
                                     start=False, stop=False, tile_position=(0, hp * 64))
                            return e.matmul(po[hp * 64:(hp + 1) * 64, oc], Sb[:, 1, blk, hp, :], qx[:, 1, bs],
                                            start=False, stop=True, tile_position=(0, hp * 64))
                        P.op("pe", om, reads=[smk, "mv", "Sb", "qx", "big"], writes=[pok])
                    emit_sc(0)
                    for i in range(len(items)):
                        if i + 1 < len(items):
                            emit_sc(i + 1)
                        emit_om(i)
                for tb in range(2):
                    tbs = slice(tb * 512, (tb + 1) * 512)
                    po, pok = pos[tb]
                    o_, ok_ = tget()
                    sq_, sqk_ = tget()
                    P.op("act", lambda e, o_=o_, po=po: e.activation(out=o_, in_=po[:], func=AF.Copy), reads=[pok, "big"], writes=[ok_])
                    P.op("act", lambda e, sq_=sq_, po=po: e.activation(out=sq_, in_=po[:], func=AF.Square), reads=[pok, "big"], writes=[sqk_])
                    pmu, pmuk = bank("aux")
                    pms, pmsk = bank("aux")
                    P.op("pe", lambda e, pmu=pmu, o_=o_: e.matmul(pmu[:], C("b64"), o_, start=True, stop=True), reads=[ok_, "cst", "big"], writes=[pmuk])
                    P.op("pe", lambda e, pms=pms, sq_=sq_: e.matmul(pms[:], C("b64"), sq_, start=True, stop=True), reads=[sqk_, "cst", "big"], writes=[pmsk])
                    P.op("dve", lambda e, o_=o_, pmu=pmu: e.tensor_tensor(out=o_, in0=o_, in1=pmu[:], op=ALU.subtract), reads=[ok_, pmuk, "big"], writes=[ok_])
                    P.op("act", lambda e, sq_=sq_, pmu=pmu: e.activation(out=sq_, in_=pmu[:], func=AF.Square), reads=[pmuk, "big"], writes=[sqk_])
                    P.op("dve", lambda e, sq_=sq_, pms=pms: e.tensor_tensor(out=sq_, in0=pms[:], in1=sq_, op=ALU.subtract), reads=[sqk_, pmsk, "big"], writes=[sqk_])
                    P.op("act", lambda e, sq_=sq_: e.activation(out=sq_, in_=sq_, func=AF.Ln, bias=EPS), reads=[sqk_, "big"], writes=[sqk_])
                    P.op("act", lambda e, sq_=sq_: e.activation(out=sq_, in_=sq_, func=AF.Exp, scale=-0.5), reads=[sqk_, "big"], writes=[sqk_])
                    P.op("dve", lambda e, o_=o_, sq_=sq_, qc=qc: e.scalar_tensor_tensor(
                        out=o_, in0=o_, scalar=vecs[:, l, 118 + qc:119 + qc], in1=sq_, op0=ALU.mult, op1=ALU.mult),
                        reads=[ok_, sqk_, vk, "big"], writes=[ok_])
                    pg, pgk = proj_fm(sg_, sgk, 0, tb)
                    P.op("act", lambda e, sq_=sq_, pg=pg: e.activation(out=sq_, in_=pg[:], func=AF.Silu), reads=[pgk, "big"], writes=[sqk_])
                    P.op("dve", lambda e, o_=o_, sq_=sq_, qc=qc, tbs=tbs: e.tensor_tensor(out=yT[:, 6 + qc, tbs], in0=o_, in1=sq_, op=ALU.mult),
                         reads=[ok_, sqk_, "big"], writes=[f"y{6 + qc}"])

        def wout_proj(g, l):
            v = 1 if g == "P" else 0
            for nh in range(2):
                sl, slk = load_slot([((lambda sl, k=k: sl[:, k:k + 4, :]),
                                      w_out[l, k * 128:(k + 4) * 128, nh * 512:(nh + 1) * 512].rearrange("(c p) n -> p c n", p=128)) for k in (0, 4)])
                for nn in range(4):
                    n = nh * 4 + nn
                    for tb in range(2):
                        tbs = slice(tb * 512, (tb + 1) * 512)
                        po, pok = bank("mm")

                        def mmo(e, po=po, sl=sl, nn=nn, tbs=tbs):
                            for k in range(8):
                                i_ = e.matmul(po[:], sl[:, k, nn * 128:(nn + 1) * 128], yT[:, k, tbs], start=(k == 0), stop=(k == 7))
                            return i_
                        P.op("pe", mmo, reads=[slk, "big"] + [f"y{k}" for k in range(8)], writes=[pok])
                        P.op("dve", lambda e, po=po, n=n, tbs=tbs: e.scalar_tensor_tensor(
                            out=xres[g][:, n, tbs], in0=po[:], scalar=coefG[:, l, v, 1, n:n + 1], in1=xres[g][:, n, tbs],
                            op0=ALU.mult, op1=ALU.add), reads=[pok, f"coef{l}", xk(g, n, tb)], writes=[xk(g, n, tb)])

        def mixer(g, l, mix):
            import os
            if g not in os.environ.get("KDBG_GROUPS", "PS"):
                mix = ""
            norm(g, l, 1)
            for c in range(8):
                P.op("dve", lambda e, c=c: e.memset(yT[:, c, :], 0.0), reads=["big"], writes=[f"y{c}"])
            if "A" in mix:
                mixer_A(g, l)
                barrier()
            if "B" in mix:
                mixer_attn(g, l, "B")
                P.muted = False
                barrier()
            if "C" in mix:
                if g == "P":
                    mixer_attn(g, l, "C")
                else:
                    mixer_nat(l)
                P.muted = False
                barrier()
            if "D" in mix:
                mixer_ret(g, l)
                barrier()
            wout_proj(g, l)
            barrier()

        import os as _os
        for l in range(depth):
            for g in ("P", "S"):
                if _os.environ.get("KDBG_SKIPFFN"):
                    break
                norm(g, l, 0)
                ffn(g, l, 0, 0)
            barrier()
            if stop == "ffn1":
                break
            for g in ("P", "S"):
                mixer(g, l, mix)
            if stop == "mix":
                break
            for g in ("P", "S"):
                norm(g, l, 2)
                ffn(g, l, 1, 2)

        barrier()
        for g in ("P", "S"):
            for blk in range(8):
                stg = stage[blk % 2]
                sk = f"stg{blk % 2}"
                for half in range(2):
                    pbk, pk = bank("aux")

                    def tr2(e, pbk=pbk, half=half, blk=blk, g=g):
                        for c4 in range(4):
                            i_ = e.transpose(pbk[:, c4 * 128:(c4 + 1) * 128],
                                             xres[g][:, half * 4 + c4, blk * 128:(blk + 1) * 128], ident)
                        return i_
                    P.op("pe", tr2, reads=["cst"] + [xk(g, c, blk // 4) for c in range(half * 4, half * 4 + 4)], writes=[pk])
                    copy_op(ew_engine(), stg[:, half * 512:(half + 1) * 512], pbk[:], [pk, "big"], [sk])
                P.op("sp", (lambda stg, dst: lambda e: e.dma_start(out=dst, in_=stg))(stg, yout[g][blk * 128:(blk + 1) * 128, :]),
                     reads=[sk], dsem="yst" + sk)
        pbk, pk = bank("aux")
        P.op("pe", lambda e, pbk=pbk: e.transpose(pbk[0:64, 0:128], lruo[:], ident), reads=["lruo", "cst"], writes=[pk])
        copy_op("dve", rows[0:64, :], pbk[0:64, 0:128], [pk], ["rows"])
        P.op("sp", lambda e: e.dma_start(out=nlru.rearrange("s l d (c p) -> (s l d c) p", p=128), in_=rows[0:64, :]),
             reads=["rows"], dsem="nlru")
        P.emit(st)
        nc._prog_stats = P.stats
    return nc


_NC_CACHE = {}


def _get_nc(depth=DEPTH, stop=None, mix="ABCD"):
    key = (depth, stop, mix)
    if key not in _NC_CACHE:
        _NC_CACHE[key] = build_program(depth, stop, mix)
    return _NC_CACHE[key]


def kernel(x_prompt, x_sample, cache_b_k, cache_b_v, cache_c_k, cache_c_v, state_lru, state_ret,
           c, c_ctx, w_mod, b_mod, norm_g, ffn_w_in, ffn_w_out, w_in, w_out, conv_w, conv_b,
           lru_w_r, lru_b_r, lru_w_i, lru_b_i, lru_lambda, gqa_qn, gqa_kn, nat_qn, nat_kn,
           nat_bias, ret_decay, ret_gn, _depth=DEPTH, _stop=None, _mix="ABCD", _ncores=8):
    f = lambda a: np.ascontiguousarray(np.asarray(a, dtype=np.float32))
    x_prompt, x_sample = f(x_prompt), f(x_sample)
    shared = dict(w_mod=f(w_mod), b_mod=f(b_mod), norm_g=f(norm_g), ffn_w_in=f(ffn_w_in), ffn_w_out=f(ffn_w_out),
                  w_in=f(w_in), w_out=f(w_out), conv_w=f(conv_w), conv_b=f(conv_b), lru_w_r=f(lru_w_r),
                  lru_b_r=f(lru_b_r), lru_w_i=f(lru_w_i), lru_b_i=f(lru_b_i), lru_lambda=f(lru_lambda),
                  gqa_qn=f(gqa_qn), gqa_kn=f(gqa_kn), nat_qn=f(nat_qn), nat_kn=f(nat_kn), nat_bias=f(nat_bias),
                  ret_decay=f(ret_decay), ret_gn=f(ret_gn), cst=_CST, cstb=_CSTB)
    c = f(c)
    c_ctx = f(c_ctx)
    cache_b_k, cache_b_v, cache_c_k, cache_c_v = f(cache_b_k), f(cache_b_v), f(cache_c_k), f(cache_c_v)
    state_lru, state_ret = f(state_lru), f(state_ret)
    in_maps = []
    for i in range(_ncores):
        b = i // 2
        m = dict(shared)
        m["xp"] = x_prompt[4 * i:4 * i + 4].reshape(T, D)
        m["xs"] = x_sample[b]
        m["cvec"] = np.stack([c[b], c_ctx], 0)
        m["cbk"] = cache_b_k[b].reshape(DEPTH, 256, 128)
        m["cbv"] = cache_b_v[b].reshape(DEPTH, 256, 128)
        m["cck"] = cache_c_k[b].reshape(DEPTH, 256, 256)
        m["ccv"] = cache_c_v[b].reshape(DEPTH, 256, 256)
        m["slru"] = state_lru[b]
        m["sret"] = state_ret[b]
        in_maps.append(m)
    nc = _get_nc(_depth, _stop, _mix)
    res = run_bass_kernel_spmd(nc, in_maps, core_ids=list(range(_ncores)))
    r = res.results
    n = _ncores
    y_prompt = np.concatenate([r[i]["yp"].reshape(4, 256, D) for i in range(n)], 0)
    y_sample = np.stack([r[2 * b]["ys"] for b in range(n // 2)], 0)
    nbk = np.concatenate([r[i]["nbk"].reshape(4, DEPTH, 256, 2, 64) for i in range(n)], 0)
    nbv = np.concatenate([r[i]["nbv"].reshape(4, DEPTH, 256, 2, 64) for i in range(n)], 0)
    nck = np.concatenate([r[i]["nck"].reshape(4, DEPTH, 256, 4, 64) for i in range(n)], 0)
    ncv = np.concatenate([r[i]["ncv"].reshape(4, DEPTH, 256, 4, 64) for i in range(n)], 0)
    nlru = np.concatenate([r[i]["nlru"] for i in range(n)], 0)
    nret = np.concatenate([r[i]["nret"] for i in range(n)], 0)
    return y_prompt, y_sample, nbk, nbv, nck, ncv, nlru, nret
```
